# Optimizing a Trainium2 kernel written in Bass

```python
import jax, jax.numpy as jnp
from jax import lax
import numpy as np

D_MODEL = 1024
BATCH = 8
SEQ = 4096
DEPTH = 2

D_CONV = D_MODEL
CONV_WIDTH = 3
RET_HEADS = 4
RET_QK_DIM = 128
RET_V_DIM = 256
RET_QK = RET_HEADS * RET_QK_DIM
RET_V = RET_HEADS * RET_V_DIM
CHUNK = 128
ROPE_BASE = 10000.0
D_FF = 4 * D_MODEL
EPS = 1e-6
MAX_START = 1024

IN_SIZES = (D_CONV, D_CONV, D_CONV, RET_QK, RET_QK, RET_V, RET_V, D_MODEL, D_MODEL)
D_IN = 3 * D_CONV + 2 * RET_QK + 2 * RET_V + 2 * D_MODEL

kernel_name = "hybrid_conv_retention_gated_block"


def rmsnorm(x, g):
    xf = x.astype(jnp.float32)
    xf = xf * lax.rsqrt(jnp.mean(xf * xf, axis=-1, keepdims=True) + EPS)
    return (xf * g.astype(jnp.float32)).astype(x.dtype)


def rotary(t, positions):
    half = t.shape[-1] // 2
    inv_freq = ROPE_BASE ** (-jnp.arange(half, dtype=jnp.float32) / half)
    ang = positions.astype(jnp.float32)[..., None] * inv_freq
    cos = jnp.cos(ang)[:, :, None, :].astype(t.dtype)
    sin = jnp.sin(ang)[:, :, None, :].astype(t.dtype)
    t1, t2 = t[..., :half], t[..., half:]
    return jnp.concatenate([t1 * cos - t2 * sin, t1 * sin + t2 * cos], axis=-1)


def short_conv_mixer(b, c, u, conv_w, w_out):
    z = c * u
    z = lax.conv_general_dilated(
        z, conv_w[:, None, :].astype(z.dtype), window_strides=(1,),
        padding=[(CONV_WIDTH - 1, 0)], dimension_numbers=("NWC", "WIO", "NWC"),
        feature_group_count=D_CONV)
    return (b * z) @ w_out


def retention_core(q, k, v, positions):
    bsz, seq = q.shape[0], q.shape[1]
    n_chunks = seq // CHUNK
    q = rotary(q, positions) * (RET_QK_DIM ** -0.5)
    k = rotary(k, positions)

    def to_chunks(t):
        return t.astype(jnp.float32).reshape(bsz, n_chunks, CHUNK, RET_HEADS, t.shape[-1]).swapaxes(0, 1)

    qc, kc, vc = to_chunks(q), to_chunks(k), to_chunks(v)
    log_gamma = jnp.log1p(-jnp.exp2(-5.0 - jnp.arange(RET_HEADS, dtype=jnp.float32)))
    pos = jnp.arange(CHUNK, dtype=jnp.float32)
    rel = pos[:, None] - pos[None, :]
    decay_intra = jnp.where(rel >= 0, jnp.exp(log_gamma[:, None, None] * jnp.maximum(rel, 0.0)), 0.0)
    decay_q = jnp.exp((pos[:, None] + 1.0) * log_gamma)
    decay_k = jnp.exp((CHUNK - 1.0 - pos[:, None]) * log_gamma)
    decay_chunk = jnp.exp(CHUNK * log_gamma)

    scores = jnp.einsum("nbqhd,nbkhd->nbhqk", qc, kc) * decay_intra
    o_intra = jnp.einsum("nbhqk,nbkhv->nbqhv", scores, vc)

    def step(state, inp):
        q_i, k_i, v_i = inp
        o_inter = jnp.einsum("bqhd,bhdv->bqhv", q_i, state) * decay_q[None, :, :, None]
        state = state * decay_chunk[None, :, None, None] + jnp.einsum("bkhd,kh,bkhv->bhdv", k_i, decay_k, v_i)
        return state, o_inter

    state0 = jnp.zeros((bsz, RET_HEADS, RET_QK_DIM, RET_V_DIM), jnp.float32)
    _, o_inter = lax.scan(step, state0, (qc, kc, vc))
    return (o_intra + o_inter).swapaxes(0, 1).reshape(bsz, seq, RET_HEADS, RET_V_DIM)


def retention_mixer(q, k, v, g, positions, ret_norm, w_out):
    bsz, seq = q.shape[0], q.shape[1]
    o = retention_core(q.reshape(bsz, seq, RET_HEADS, RET_QK_DIM),
                       k.reshape(bsz, seq, RET_HEADS, RET_QK_DIM),
                       v.reshape(bsz, seq, RET_HEADS, RET_V_DIM), positions)
    mu = jnp.mean(o, axis=-1, keepdims=True)
    var = jnp.mean(jnp.square(o - mu), axis=-1, keepdims=True)
    o = ((o - mu) * lax.rsqrt(var + EPS)).reshape(bsz, seq, RET_V) * ret_norm.astype(jnp.float32)
    o = o.astype(g.dtype)
    return (jax.nn.silu(g) * o) @ w_out


def hybrid_layer(x, positions, norm_mix, w_in, conv_w, w_conv_out, ret_norm, w_ret_out, w_o,
                 norm_mlp, w_up, w_down):
    h = rmsnorm(x, norm_mix)
    proj = h @ w_in
    split_points = np.cumsum(IN_SIZES)[:-1].tolist()
    cb, cc, cu, q, k, v, g, ga, gb = jnp.split(proj, split_points, axis=-1)
    y_conv = short_conv_mixer(cb, cc, cu, conv_w, w_conv_out)
    y_ret = retention_mixer(q, k, v, g, positions, ret_norm, w_ret_out)
    merged = jax.nn.sigmoid(ga) * y_conv + jax.nn.sigmoid(gb) * y_ret
    x = x + merged @ w_o
    h2 = rmsnorm(x, norm_mlp)
    return x + jnp.square(jax.nn.relu(h2 @ w_up)) @ w_down


def setup_inputs(seed: int = 0) -> dict:
    key = jax.random.key(seed)
    ks = jax.random.split(key, 14)
    f32 = jnp.float32

    def dense(k, fan_in, fan_out):
        return jax.random.normal(k, (DEPTH, fan_in, fan_out), f32) * (fan_in ** -0.5)

    def gain(k, shape):
        return 1.0 + 0.02 * jax.random.normal(k, shape, f32)

    x = jax.random.normal(ks[0], (BATCH, SEQ, D_MODEL), f32)
    start = jax.random.randint(ks[1], (BATCH, 1), 0, MAX_START, dtype=jnp.int32)
    positions = (start + jnp.arange(SEQ, dtype=jnp.int32)[None, :]).astype(jnp.int32)
    return {
        "x": x,
        "positions": positions,
        "norm_mix": gain(ks[2], (DEPTH, D_MODEL)),
        "w_in": dense(ks[3], D_MODEL, D_IN),
        "conv_w": jax.random.normal(ks[4], (DEPTH, CONV_WIDTH, D_CONV), f32) * (CONV_WIDTH ** -0.5),
        "w_conv_out": dense(ks[5], D_CONV, D_MODEL),
        "ret_norm": gain(ks[6], (DEPTH, RET_V)),
        "w_ret_out": dense(ks[7], RET_V, D_MODEL),
        "w_o": dense(ks[8], D_MODEL, D_MODEL),
        "norm_mlp": gain(ks[9], (DEPTH, D_MODEL)),
        "w_up": dense(ks[10], D_MODEL, D_FF),
        "w_down": dense(ks[11], D_FF, D_MODEL),
        "norm_final": gain(ks[12], (D_MODEL,)),
    }


def reference(x, positions, norm_mix, w_in, conv_w, w_conv_out, ret_norm, w_ret_out, w_o,
              norm_mlp, w_up, w_down, norm_final):
    for i in range(DEPTH):
        x = hybrid_layer(x, positions, norm_mix[i], w_in[i], conv_w[i], w_conv_out[i], ret_norm[i],
                         w_ret_out[i], w_o[i], norm_mlp[i], w_up[i], w_down[i])
    return rmsnorm(x, norm_final)
```

```python
import numpy as np
from contextlib import ExitStack
import concourse.bass as bass
import concourse.mybir as mybir
from concourse.bass_utils import run_bass_kernel_spmd

F32 = mybir.dt.float32
BF16 = mybir.dt.bfloat16
I32 = mybir.dt.int32
AF = mybir.ActivationFunctionType
ALU = mybir.AluOpType
PI = float(np.pi)

P = 128
T = 512
G = 4
D = 1024
L = 2
H = 4
EPS = 1e-6
NB = 4
NBLK = 38
CAST_AHEAD = 10
N_CAST_SEM = 12

O_CB, O_CC, O_CU, O_Q, O_K, O_V, O_G, O_GA, O_GB = 0, 1024, 2048, 3072, 3584, 4096, 5120, 6144, 7168

C_ID = 0
C_MASK = 128
C_DQ = 640
C_DK = 656
C_INVF = 672
C_MH = 736
NCONST = 740
GC4 = [float((1.0 - 2.0 ** (-5.0 - h)) ** 512.0) for h in range(4)]
PL = 48
P_NF = L * PL
NPAR = P_NF + 1024


class Tok:
    __slots__ = ("sem", "val")

    def __init__(self, sem, val):
        self.sem = sem
        self.val = val


class Res:
    __slots__ = ("name", "w", "r", "excl")

    def __init__(self, name="", excl=False):
        self.name = name
        self.w = None
        self.r = {}
        self.excl = excl


class Sched:
    ENG = ("pe", "act", "dve", "pool", "sp")

    def __init__(self, nc, stack):
        self.nc = nc
        self.eng = {"pe": nc.tensor, "act": nc.scalar, "dve": nc.vector, "pool": nc.gpsimd, "sp": nc.sync}
        self.sem = {e: stack.enter_context(nc.semaphore("s_" + e)) for e in self.ENG}
        self.cnt = {e: 0 for e in self.ENG}
        self.known = {e: {} for e in self.ENG}
        self.dsem, self.dcnt, self.dlast = {}, {}, {}
        self.stack = stack
        self.nwaits = 0
        self.nops = {e: 0 for e in self.ENG}

    def _wait(self, e, tok):
        if tok is None:
            return
        k = self.known[e]
        key = tok.sem.num
        if k.get(key, 0) >= tok.val:
            return
        self.eng[e].wait_ge(tok.sem, tok.val)
        self.nwaits += 1
        k[key] = tok.val

    def _deps(self, e, reads, writes):
        own = self.sem[e]
        for r in reads:
            self._wait(e, r.w)
            if r.excl:
                for t in list(r.r.values()):
                    if t.sem is not own:
                        self._wait(e, t)
        for w in writes:
            self._wait(e, w.w)
            for t in list(w.r.values()):
                self._wait(e, t)

    def _commit(self, tok, reads, writes):
        key = tok.sem.num
        for r in reads:
            old = r.r.get(key)
            if old is None or old.val < tok.val:
                r.r[key] = tok
        for w in writes:
            w.w = tok
            w.r = {}

    def op(self, e, reads, writes, fn):
        self._deps(e, reads, writes)
        ins = fn(self.eng[e])
        self.cnt[e] += 1
        self.nops[e] += 1
        ins.then_inc(self.sem[e], 1)
        tok = Tok(self.sem[e], self.cnt[e])
        self._commit(tok, reads, writes)
        return tok

    def dma(self, q, semkey, reads, writes, fns):
        if semkey not in self.dsem:
            self.dsem[semkey] = self.stack.enter_context(self.nc.semaphore("d_" + str(semkey)))
            self.dcnt[semkey] = 0
            self.dlast[semkey] = None
        sem = self.dsem[semkey]
        self._wait(q, self.dlast[semkey])
        self._deps(q, reads, writes)
        for fn in fns:
            fn(self.eng[q]).then_inc(sem, 16)
            self.dcnt[semkey] += 16
        tok = Tok(sem, self.dcnt[semkey])
        self.dlast[semkey] = tok
        self._commit(tok, reads, writes)
        return tok


def layer_blocks():
    blocks = []
    chunks = []
    for c in range(8):
        chunks += [("w_in", 0, O_CC + c * 128, 128), ("w_in", 0, O_CU + c * 128, 128), ("w_in", 0, O_CB + c * 128, 128)]
    for j in range(3):
        blocks.append(chunks[4 * j:4 * j + 4])
    for off in (O_Q, O_K, O_V, O_V + 512):
        blocks.append([("w_in", 0, off, 512)])
    for j in range(3, 6):
        blocks.append(chunks[4 * j:4 * j + 4])
    for off in (O_G, O_G + 512):
        blocks.append([("w_in", 0, off, 512)])
    for c in range(8):
        blocks.append([("w_conv_out", 0, c * 128, 128), ("w_ret_out", 0, c * 128, 128),
                       ("w_in", 0, O_GA + c * 128, 128), ("w_in", 0, O_GB + c * 128, 128)])
    for cb in range(2):
        blocks.append([("w_o", 0, cb * 512, 512)])
    for hf in range(2):
        for j in range(4):
            blocks.append([("w_up", 0, (hf * 4 + j) * 512, 512)])
        for cb in range(2):
            for kg in range(2):
                blocks.append([("w_down", (hf * 16 + kg * 8) * 128, cb * 512, 512)])
    assert len(blocks) == NBLK
    return blocks


B_CONVA, B_B2, B_CONVB, B_G, B_E, B_WO, B_MLP = 0, 3, 7, 10, 12, 20, 22


def build(ntiles=8, nlayers=L, dbg=None):
    S_TOK = ntiles * T
    nc = bass.Bass("TRN2", target_bir_lowering=False)
    x_d = nc.dram_tensor("x", [S_TOK, D], F32, kind="ExternalInput").ap()
    pos_d = nc.dram_tensor("pos", [1, S_TOK], I32, kind="ExternalInput").ap()
    cst_d = nc.dram_tensor("cst", [P, NCONST], F32, kind="ExternalInput").ap()
    par_d = nc.dram_tensor("par", [P, NPAR], F32, kind="ExternalInput").ap()
    wd = {
        "w_in": nc.dram_tensor("w_in", [L, D, 8192], F32, kind="ExternalInput").ap(),
        "w_conv_out": nc.dram_tensor("w_conv_out", [L, D, D], F32, kind="ExternalInput").ap(),
        "w_ret_out": nc.dram_tensor("w_ret_out", [L, D, D], F32, kind="ExternalInput").ap(),
        "w_o": nc.dram_tensor("w_o", [L, D, D], F32, kind="ExternalInput").ap(),
        "w_up": nc.dram_tensor("w_up", [L, D, 4096], F32, kind="ExternalInput").ap(),
        "w_down": nc.dram_tensor("w_down", [L, 4096, D], F32, kind="ExternalInput").ap(),
    }
    out_d = nc.dram_tensor("out", [S_TOK, D], F32, kind="ExternalOutput").ap()
    wbf_d = nc.dram_tensor("wbf", [L * NBLK * P, 4096], BF16).ap()
    dbg_d = None
    if dbg is not None:
        dbg_d = nc.dram_tensor("dbg", [P, dbg], F32, kind="ExternalOutput").ap()

    LB = layer_blocks()

    with ExitStack() as st:
        S = Sched(nc, st)
        sb = lambda name, shape, dt: st.enter_context(nc.sbuf_tensor("sb_" + name, shape, dt))

        cst = sb("cst", [P, NCONST], F32)
        par = sb("par", [P, NPAR], F32)
        ident = sb("ident", [P, P], BF16)
        fm = sb("fm", [P, 32, T], BF16)
        x_sbs = [sb("x_sb%d" % i, [P, G, D], F32) for i in range(2)]
        xs_bf = [sb("xs%d" % i, [P, D], BF16) for i in range(2)]
        wring = [sb("wr%d" % i, [P, 4096], BF16) for i in range(NB)]
        scr = [sb("scr%d" % i, [P, T], F32) for i in range(6)]
        zbuf = [sb("zb%d" % i, [P, T + 2], F32) for i in range(2)]
        halo = sb("halo", [P, L, 8, 2], F32)
        qr = sb("qr", [P, G, 512], BF16)
        kr = sb("kr", [P, G, 512], BF16)
        v_bf = sb("v_bf", [P, G, D], BF16)
        gg_bf = sb("gg_bf", [P, G, D], BF16)
        tabs = {k: sb("tab_" + k, [P, G, H, 64], F32) for k in ("cq", "sq", "ck", "sk")}
        cosb = sb("cosb", [P, G, 64], F32)
        sinb = sb("sinb", [P, G, 64], F32)
        posi = sb("posi", [1, T], I32)
        posf = sb("posf", [1, T], F32)
        qkT = [sb("qkT%d" % i, [P, 1024], BF16) for i in range(2)]
        sT = [sb("sT%d" % i, [P, 512], BF16) for i in range(2)]
        stloc = [sb("stloc%d" % i, [P, D], BF16) for i in range(2)]
        st32 = [sb("st32_%d" % l, [P, H, 256], F32) for l in range(L)]
        stbf = [sb("stbf_%d" % l, [P, D], BF16) for l in range(L)]
        on32 = [sb("on32_%d" % i, [P, D], F32) for i in range(2)]
        gated = [sb("gated%d" % i, [P, D], BF16) for i in range(2)]
        bnst = sb("bnst", [P, H, 6], F32)
        mv = [sb("mv%d" % i, [P, H, 2], F32) for i in range(2)]
        gn_r = [sb("gn_r%d" % i, [P, H], F32) for i in range(2)]
        ssq = sb("ssq", [P, G], F32)
        rstd = sb("rstd", [P, G], F32)
        ssq2 = sb("ssq2", [P, G], F32)
        rstd2 = sb("rstd2", [P, G], F32)
        banks = [st.enter_context(nc.psum_tensor("bk%d" % i, [P, 512], F32)) for i in range(8)]

        R_cst, R_par, R_ident = Res("cst"), Res("par"), Res("ident")
        R_fm = [[Res("fm%d_%d" % (c, g)) for g in range(G)] for c in range(32)]
        R_xs2 = [[[Res("x%d_%d_%d" % (i, g, cb)) for cb in range(2)] for g in range(G)] for i in range(2)]
        R_xs = [Res("xs0"), Res("xs1")]
        R_slot = [Res("slot%d" % i) for i in range(NB)]
        R_wbf = [Res("wbf%d" % i) for i in range(L * NBLK)]
        R_scr = [Res("scr%d" % i) for i in range(6)]
        R_zb = [Res("zb0"), Res("zb1")]
        R_halo = [[Res("halo%d_%d" % (l, c)) for c in range(8)] for l in range(L)]
        R_qr = [Res("qr%d" % g) for g in range(G)]
        R_kr = [Res("kr%d" % g) for g in range(G)]
        R_v = [[Res("v%d_%d" % (g, b)) for b in range(2)] for g in range(G)]
        R_gg = [[Res("gg%d_%d" % (g, b)) for b in range(2)] for g in range(G)]
        R_tab = Res("tabs")
        R_trig = Res("trig")
        R_posi, R_posf = Res("posi"), Res("posf")
        R_qT, R_kT, R_sT = [Res(), Res()], [Res(), Res()], [Res(), Res()]
        R_st32 = [Res("st32_0"), Res("st32_1")]
        R_stbf = [Res("stbf0"), Res("stbf1")]
        R_stloc = [Res("stloc0"), Res("stloc1")]
        R_on = [Res("on0"), Res("on1")]
        R_gated = [Res("gated0"), Res("gated1")]
        R_gn = Res("gnstats")
        R_gn2 = [Res("gnrb0"), Res("gnrb1")]
        R_ssq = Res("ssq")
        R_ssq2 = Res("ssq2")
        R_bank = [Res("bank%d" % i, excl=True) for i in range(8)]
        R_out = Res("out")
        R_dbg = Res("dbg")

        free_banks = list(range(8))

        def balloc():
            assert free_banks, "out of PSUM banks"
            return free_banks.pop(0)

        def bfree(b):
            free_banks.append(b)

        free_scr = list(range(len(scr)))

        def salloc():
            assert free_scr, "out of scratch buffers"
            return free_scr.pop(0)

        def sfree(*idx):
            for i in idx:
                free_scr.append(i)

        alt = [0]

        def alt_eng():
            alt[0] ^= 1
            return "act" if alt[0] else "dve"

        total_stream = ntiles * nlayers * NBLK
        wst = {"next_load": 0, "next_cast": 0}
        ncast_total = nlayers * NBLK

        def emit_cast(sid):
            l, j = divmod(sid, NBLK)
            fns = []
            off = 0
            for (name, row0, col0, ncols) in LB[j]:
                src = wd[name][l, row0:row0 + 8 * P, col0:col0 + ncols].rearrange("(kc p) c -> p kc c", p=P)
                dst = wbf_d[sid * P:(sid + 1) * P, :].rearrange("p (kc c) -> p kc c", kc=8)[:, :, off:off + ncols]
                fns.append(lambda e, src=src, dst=dst: e.dma_start(out=dst, in_=src))
                off += ncols
            S.dma("pool", "cast%d" % (sid % N_CAST_SEM), [], [R_wbf[sid]], fns)

        def ensure_cast(upto):
            while wst["next_cast"] <= min(upto, ncast_total - 1):
                emit_cast(wst["next_cast"])
                wst["next_cast"] += 1

        def stream_sid(i):
            return ((i // NBLK) % nlayers) * NBLK + (i % NBLK)

        def emit_load(i):
            sid = stream_sid(i)
            ensure_cast(sid + CAST_AHEAD)
            slot = i % NB
            S.dma("sp", "slot%d" % slot, [R_wbf[sid]], [R_slot[slot]],
                  [lambda e, slot=slot, sid=sid: e.dma_start(out=wring[slot][:], in_=wbf_d[sid * P:(sid + 1) * P, :])])

        def wget(i):
            while wst["next_load"] <= i:
                assert wst["next_load"] < NB, "weight block requested before its slot was released"
                emit_load(wst["next_load"])
                wst["next_load"] += 1
            return i % NB

        def wrelease(i):
            nxt = i + NB
            assert wst["next_load"] == nxt, (wst["next_load"], nxt)
            if nxt < total_stream:
                emit_load(nxt)
            wst["next_load"] = nxt + 1

        S.dma("sp", "c0", [], [R_cst], [lambda e: e.dma_start(out=cst[:], in_=cst_d)])
        S.dma("sp", "c1", [], [R_par], [lambda e: e.dma_start(out=par[:], in_=par_d)])
        S.op("dve", [R_cst], [R_ident], lambda e: e.tensor_copy(out=ident[:], in_=cst[:, C_ID:C_ID + 128]))
        for l in range(L):
            S.op("dve", [], [R_st32[l]], lambda e, l=l: e.memset(st32[l][:], 0.0))
            S.op("dve", [], [R_stbf[l]], lambda e, l=l: e.memset(stbf[l][:], 0.0))
        S.op("dve", [], [r for rr in R_halo for r in rr], lambda e: e.memset(halo[:], 0.0))
        for i in range(NB):
            wget(i)

        maskT = cst[:, C_MASK:C_MASK + 512]
        mhalf = cst[:, C_MH:C_MH + 1]

        def pe_group(reads, bank, fn):
            return S.op("pe", reads, [R_bank[bank]], fn)

        def rms_stats(xb, g):
            k = g % 2
            x_sb, R_x = x_sbs[xb], R_xs2[xb]
            S.op("act", R_x[g], [R_xs[k], R_ssq], lambda e: e.activation(
                out=xs_bf[k][:], in_=x_sb[:, g, :], func=AF.Square, accum_out=ssq[:, g:g + 1]))
            S.op("dve", [R_ssq], [R_ssq], lambda e: e.tensor_scalar(
                out=rstd[:, g:g + 1], in0=ssq[:, g:g + 1], scalar1=1.0 / D, scalar2=EPS, op0=ALU.mult, op1=ALU.add))
            S.op("pool", [R_ssq, R_cst], [R_ssq], lambda e: e.tensor_tensor(
                out=rstd[:, g:g + 1], in0=rstd[:, g:g + 1], in1=mhalf, op=ALU.pow))

        def rms_stats2(xb):
            x_sb, R_x = x_sbs[xb], R_xs2[xb]
            for g in range(G):
                k = g % 2
                S.op("act", R_x[g], [R_gated[k], R_ssq2], lambda e, g=g, k=k: e.activation(
                    out=gated[k][:], in_=x_sb[:, g, :], func=AF.Square, accum_out=ssq2[:, g:g + 1]))
            S.op("dve", [R_ssq2], [R_ssq2], lambda e: e.tensor_scalar(
                out=rstd2[:], in0=ssq2[:], scalar1=1.0 / D, scalar2=EPS, op0=ALU.mult, op1=ALU.add))
            S.op("dve", [R_ssq2], [R_ssq2], lambda e: e.reciprocal(out=rstd2[:], in_=rstd2[:]))

        def make_norm(xb, gain_off, raw=False):
            x_sb, R_x = x_sbs[xb], R_xs2[xb]

            def pre(g):
                k = g % 2
                if raw:
                    S.op("act", R_x[g], [R_xs[k]], lambda e: e.activation(out=xs_bf[k][:], in_=x_sb[:, g, :], func=AF.Copy))
                    return
                rms_stats(xb, g)
                S.op("act", R_x[g] + [R_ssq], [R_xs[k]], lambda e: e.activation(
                    out=xs_bf[k][:], in_=x_sb[:, g, :], func=AF.Copy, scale=rstd[:, g:g + 1]))

            def tr(g):
                k = g % 2
                for half in range(2):
                    b = balloc()
                    tp = banks[b][:].bitcast(BF16)

                    def trf(e, half=half, tp=tp):
                        ins = None
                        for j in range(4):
                            c = half * 4 + j
                            ins = e.transpose(out=tp[:, j * P:(j + 1) * P], in_=xs_bf[k][:, c * P:(c + 1) * P], identity=ident[:])
                        return ins
                    pe_group([R_xs[k], R_ident], b, trf)
                    S.op("dve", [R_bank[b], R_par], [R_fm[half * 4 + j][g] for j in range(4)],
                         lambda e, half=half, tp=tp: e.tensor_tensor(
                             out=fm[:, half * 4:half * 4 + 4, g * P:(g + 1) * P],
                             in0=tp[:, 0:512].rearrange("p (j t) -> p j t", j=4),
                             in1=par[:, gain_off + half * 4:gain_off + half * 4 + 4].unsqueeze(2).to_broadcast([P, 4, P]),
                             op=ALU.mult))
                    bfree(b)
            return pre, tr

        def norm_full(xb, gain_off):
            pre, tr = make_norm(xb, gain_off)
            for g in range(G):
                pre(g)
                tr(g)

        def mm_fm(bank, slot, sub, kbase_chunk, src_chunk0, split=False):
            parts = [(0, 4), (4, 8)] if split else [(0, 8)]
            tok = None
            for (lo, hi) in parts:
                def fn(e, lo=lo, hi=hi):
                    ins = None
                    for kc in range(lo, hi):
                        ins = e.matmul(banks[bank][:, :], lhsT=wring[slot][:, kc * 512 + sub * P: kc * 512 + (sub + 1) * P],
                                       rhs=fm[:, src_chunk0 + kc, :], start=(kc == 0), stop=(kc == 7))
                    return ins
                reads = [R_slot[slot]] + [R_fm[src_chunk0 + kc][g] for kc in range(lo, hi) for g in range(G)]
                tok = pe_group(reads, bank, fn)
            return tok

        def mm_tm(bank, slots, src_chunk0, g, split=False):
            n = len(slots) * 8
            parts = [(0, n // 2), (n // 2, n)] if split else [(0, n)]
            tok = None
            for (lo, hi) in parts:
                def fn(e, lo=lo, hi=hi):
                    ins = None
                    for i in range(lo, hi):
                        si, kc = divmod(i, 8)
                        ins = e.matmul(banks[bank][:, :], lhsT=fm[:, src_chunk0 + i, g * P:(g + 1) * P],
                                       rhs=wring[slots[si]][:, kc * 512:(kc + 1) * 512], start=(i == 0), stop=(i == n - 1))
                    return ins
                reads = [R_slot[slots[i // 8]] for i in range(lo, hi, 8)] + [R_fm[src_chunk0 + i][g] for i in range(lo, hi)]
                tok = pe_group(reads, bank, fn)
            return tok

        def tables_a(it):
            S.dma("sp", "pos", [], [R_posi], [lambda e: e.dma_start(out=posi[:], in_=pos_d[:, it * T:(it + 1) * T])])
            S.op("dve", [R_posi], [R_posf], lambda e: e.tensor_copy(out=posf[:], in_=posi[:]))

        def tables_b(it):
            b = balloc()

            def fn(e):
                ins = None
                for g in range(G):
                    ins = e.matmul(banks[b][:, g * 64:(g + 1) * 64], lhsT=posf[0:1, g * P:(g + 1) * P],
                                   rhs=cst[0:1, C_INVF:C_INVF + 64], start=True, stop=True)
                return ins
            pe_group([R_posf, R_cst], b, fn)
            ia, ib, ic = salloc(), salloc(), salloc()
            Ra, Rb, Rc = R_scr[ia], R_scr[ib], R_scr[ic]
            angf = scr[ia][:, 0:256]
            angm = scr[ia][:, 256:512]
            angr = scr[ib][:, :].rearrange("p (t n) -> p t n", t=2)
            angk = scr[ic][:, 0:256].bitcast(I32)
            ang = banks[b][:, 0:G * 64]
            S.op("dve", [R_bank[b]], [Ra], lambda e: e.tensor_copy(out=angf, in_=ang))
            bfree(b)
            S.op("dve", [Ra], [Rc], lambda e: e.tensor_scalar(out=angk, in0=angf, scalar1=1.0 / (2 * PI), scalar2=None, op0=ALU.mult))
            S.op("dve", [Rc], [Ra], lambda e: e.tensor_copy(out=angm, in_=angk))
            S.op("dve", [Ra], [Rb], lambda e: e.scalar_tensor_tensor(out=angr[:, 0, :], in0=angm, scalar=-2 * PI, in1=angf, op0=ALU.mult, op1=ALU.add))
            S.op("dve", [Rb], [Ra], lambda e: e.tensor_scalar(out=angm, in0=angr[:, 0, :], scalar1=PI, scalar2=-2 * PI, op0=ALU.is_gt, op1=ALU.mult))
            S.op("dve", [Ra, Rb], [Rb], lambda e: e.tensor_tensor(out=angr[:, 0, :], in0=angr[:, 0, :], in1=angm, op=ALU.add))
            S.op("dve", [Rb], [Rb], lambda e: e.tensor_scalar(out=angr[:, 1, :], in0=angr[:, 0, :], scalar1=0.5 * PI, scalar2=None, op0=ALU.add))
            S.op("dve", [Rb], [Ra], lambda e: e.tensor_scalar(out=angm, in0=angr[:, 1, :], scalar1=PI, scalar2=-2 * PI, op0=ALU.is_gt, op1=ALU.mult))
            S.op("dve", [Ra, Rb], [Rb], lambda e: e.tensor_tensor(out=angr[:, 1, :], in0=angr[:, 1, :], in1=angm, op=ALU.add))
            S.op("dve", [Rb], [Rb], lambda e: e.tensor_scalar(out=angr, in0=angr, scalar1=-PI, scalar2=PI, op0=ALU.max, op1=ALU.min))
            S.op("act", [Rb], [R_trig], lambda e: e.activation(out=sinb[:].rearrange("p g i -> p (g i)"), in_=angr[:, 0, :], func=AF.Sin))
            S.op("act", [Rb], [R_trig], lambda e: e.activation(out=cosb[:].rearrange("p g i -> p (g i)"), in_=angr[:, 1, :], func=AF.Sin))
            sfree(ia, ib, ic)
            for h in range(H):
                for (tn, base, col) in (("cq", cosb, C_DQ), ("sq", sinb, C_DQ), ("ck", cosb, C_DK), ("sk", sinb, C_DK)):
                    S.op("dve", [R_trig, R_cst], [R_tab], lambda e, tn=tn, base=base, col=col, h=h: e.tensor_tensor(
                        out=tabs[tn][:, :, h, :], in0=base[:],
                        in1=cst[:, col + h * 4:col + h * 4 + 4].unsqueeze(2).to_broadcast([P, G, 64]), op=ALU.mult))

        def conv_chunk(l, c, wi):
            pw = l * PL + 24
            if True:
                sl = {}
                for kind in range(3):
                    n = 3 * (c % 4) + kind
                    sl[kind] = (wget(wi + n // 4), n % 4)
                b_cc = balloc()
                mm_fm(b_cc, sl[0][0], sl[0][1], 0, 0, split=(c == 0))
                i_cc = salloc()
                S.op("act", [R_bank[b_cc]], [R_scr[i_cc]], lambda e, b=b_cc, i=i_cc: e.activation(out=scr[i][:], in_=banks[b][:, :], func=AF.Copy))
                bfree(b_cc)
                b_cu = balloc()
                mm_fm(b_cu, sl[1][0], sl[1][1], 0, 0)
                zk = c % 2
                S.op("dve", [R_bank[b_cu], R_scr[i_cc]], [R_zb[zk]], lambda e, b=b_cu, i=i_cc, zk=zk: e.tensor_tensor(
                    out=zbuf[zk][:, 2:T + 2], in0=banks[b][:, :], in1=scr[i][:], op=ALU.mult))
                bfree(b_cu)
                sfree(i_cc)
                S.op("pool", [R_halo[l][c]], [R_zb[zk]], lambda e, zk=zk, c=c: e.tensor_copy(out=zbuf[zk][:, 0:2], in_=halo[:, l, c, :]))
                i_acc = salloc()
                S.op("dve", [R_zb[zk], R_par], [R_scr[i_acc]], lambda e, zk=zk, i=i_acc, c=c: e.tensor_scalar(
                    out=scr[i][:], in0=zbuf[zk][:, 2:T + 2], scalar1=par[:, pw + c * 3 + 2:pw + c * 3 + 3], scalar2=None, op0=ALU.mult))
                S.op("dve", [R_zb[zk], R_par, R_scr[i_acc]], [R_scr[i_acc]], lambda e, zk=zk, i=i_acc, c=c: e.scalar_tensor_tensor(
                    out=scr[i][:], in0=zbuf[zk][:, 1:T + 1], scalar=par[:, pw + c * 3 + 1:pw + c * 3 + 2], in1=scr[i][:], op0=ALU.mult, op1=ALU.add))
                S.op("dve", [R_zb[zk], R_par, R_scr[i_acc]], [R_scr[i_acc]], lambda e, zk=zk, i=i_acc, c=c: e.scalar_tensor_tensor(
                    out=scr[i][:], in0=zbuf[zk][:, 0:T], scalar=par[:, pw + c * 3:pw + c * 3 + 1], in1=scr[i][:], op0=ALU.mult, op1=ALU.add))
                S.op("pool", [R_zb[zk]], [R_halo[l][c]], lambda e, zk=zk, c=c: e.tensor_copy(out=halo[:, l, c, :], in_=zbuf[zk][:, T:T + 2]))
                b_cb = balloc()
                mm_fm(b_cb, sl[2][0], sl[2][1], 0, 0)
                S.op("dve", [R_bank[b_cb], R_scr[i_acc]], R_fm[8 + c], lambda e, b=b_cb, i=i_acc, c=c: e.tensor_tensor(
                    out=fm[:, 8 + c, :], in0=banks[b][:, :], in1=scr[i][:], op=ALU.mult))
                bfree(b_cb)
                sfree(i_acc)
                for kind in range(3):
                    n = 3 * (c % 4) + kind
                    if n % 4 == 3:
                        wrelease(wi + n // 4)

        def rotary(bank, g, dst, R_dst, tc, ts):
            src = banks[bank][:, :].rearrange("p (h t i) -> p h t i", h=H, t=2)
            t1 = src[:, :, 0, :]
            t2 = src[:, :, 1, :]
            out4 = dst[:, g, :].rearrange("p (h t i) -> p h t i", h=H, t=2)
            ia, ib = salloc(), salloc()
            a = scr[ia][:, 0:256].rearrange("p (h i) -> p h i", h=H)
            a2 = scr[ia][:, 256:512].rearrange("p (h i) -> p h i", h=H)
            bb = scr[ib][:, 0:256].rearrange("p (h i) -> p h i", h=H)
            b2 = scr[ib][:, 256:512].rearrange("p (h i) -> p h i", h=H)
            cosT = tabs[tc][:, g, :, :]
            sinT = tabs[ts][:, g, :, :]
            S.op("dve", [R_bank[bank], R_tab], [R_scr[ia]], lambda e: e.tensor_tensor(out=a, in0=t1, in1=cosT, op=ALU.mult))
            S.op("dve", [R_bank[bank], R_tab], [R_scr[ib]], lambda e: e.tensor_tensor(out=bb, in0=t2, in1=sinT, op=ALU.mult))
            S.op("dve", [R_bank[bank], R_tab], [R_scr[ia]], lambda e: e.tensor_tensor(out=a2, in0=t1, in1=sinT, op=ALU.mult))
            S.op("dve", [R_bank[bank], R_tab], [R_scr[ib]], lambda e: e.tensor_tensor(out=b2, in0=t2, in1=cosT, op=ALU.mult))
            S.op("pool", [R_scr[ia], R_scr[ib]], [R_dst], lambda e: e.tensor_tensor(out=out4[:, :, 0, :], in0=a, in1=bb, op=ALU.subtract))
            S.op("pool", [R_scr[ia], R_scr[ib]], [R_dst], lambda e: e.tensor_tensor(out=out4[:, :, 1, :], in0=a2, in1=b2, op=ALU.add))
            sfree(ia, ib)

        def stage_b2(l, wi):
            for j in range(4):
                slot = wget(wi + j)
                for g in range(G):
                    b = balloc()
                    mm_tm(b, [slot], 0, g)
                    if j == 0:
                        rotary(b, g, qr, R_qr[g], "cq", "sq")
                    elif j == 1:
                        rotary(b, g, kr, R_kr[g], "ck", "sk")
                    else:
                        vb = j - 2
                        S.op("act", [R_bank[b]], [R_v[g][vb]], lambda e, b=b, g=g, vb=vb: e.activation(
                            out=v_bf[:, g, vb * 512:(vb + 1) * 512], in_=banks[b][:, :], func=AF.Copy))
                    bfree(b)
                wrelease(wi + j)

        def gproj_unit(l, wi, gb_, g):
            slot = wget(wi + gb_)
            b = balloc()
            mm_tm(b, [slot], 0, g)
            i_s = salloc()
            S.op("act", [R_bank[b]], [R_scr[i_s]], lambda e: e.activation(out=scr[i_s][:], in_=banks[b][:, :], func=AF.Sigmoid))
            S.op("dve", [R_bank[b], R_scr[i_s]], [R_gg[g][gb_]], lambda e: e.tensor_tensor(
                out=gg_bf[:, g, gb_ * 512:(gb_ + 1) * 512], in0=banks[b][:, :], in1=scr[i_s][:], op=ALU.mult))
            bfree(b)
            sfree(i_s)
            if g == G - 1:
                wrelease(wi + gb_)

        def stage_ret(l, fillers=None):
            pr = l * PL + 8
            st_ = {}

            def A(g):
                k2 = g % 2
                b = balloc()
                tp = banks[b][:].bitcast(BF16)

                def tr(e):
                    ins = None
                    for h in range(H):
                        ins = e.transpose(out=tp[:, h * P:(h + 1) * P], in_=qr[:, g, h * P:(h + 1) * P], identity=ident[:])
                    for h in range(H):
                        ins = e.transpose(out=tp[:, 512 + h * P:512 + (h + 1) * P], in_=kr[:, g, h * P:(h + 1) * P], identity=ident[:])
                    return ins
                pe_group([R_qr[g], R_kr[g], R_ident], b, tr)
                S.op("act", [R_bank[b]], [R_qT[k2]], lambda e: e.activation(out=qkT[k2][:], in_=tp[:, 0:1024], func=AF.Copy))
                bfree(b)

            def B(g):
                k2 = g % 2
                b_s = balloc()

                def sc(e):
                    ins = None
                    for h in range(H):
                        ins = e.matmul(banks[b_s][:, h * P:(h + 1) * P], lhsT=qkT[k2][:, 512 + h * P:512 + (h + 1) * P],
                                       rhs=qkT[k2][:, h * P:(h + 1) * P], start=True, stop=True)
                    return ins
                pe_group([R_qT[k2]], b_s, sc)
                S.op("dve", [R_bank[b_s], R_cst], [R_sT[k2]], lambda e: e.tensor_tensor(out=sT[k2][:], in0=banks[b_s][:, :], in1=maskT, op=ALU.mult))
                bfree(b_s)

            b_S = [balloc(), balloc()]

            def Dm(g):
                for hp in range(2):
                    def spf(e, hp=hp):
                        ins = None
                        for hh in range(2):
                            h = hp * 2 + hh
                            first = (g == 0 and hh == 0)
                            ins = e.matmul(banks[b_S[hp]][:, hh * 256:(hh + 1) * 256], lhsT=kr[:, g, h * P:(h + 1) * P],
                                           rhs=v_bf[:, g, h * 256:(h + 1) * 256], start=first, stop=True, skip_group_check=(not first))
                        return ins
                    pe_group([R_kr[g], R_v[g][hp]], b_S[hp], spf)

            def C(g):
                k2 = g % 2
                b_o = [balloc(), balloc()]
                st_[("o", g)] = b_o
                sl = stloc[(g - 1) % 2]
                for hp in range(2):
                    def of(e, hp=hp):
                        ins = None
                        for hh in range(2):
                            h = hp * 2 + hh
                            o_ap = banks[b_o[hp]][:, hh * 256:(hh + 1) * 256]
                            e.matmul(o_ap, lhsT=sT[k2][:, h * P:(h + 1) * P], rhs=v_bf[:, g, h * 256:(h + 1) * 256], start=True, stop=False)
                            ins = e.matmul(o_ap, lhsT=qkT[k2][:, h * P:(h + 1) * P], rhs=stbf[l][:, h * 256:(h + 1) * 256], start=False, stop=(g == 0))
                            if g > 0:
                                ins = e.matmul(o_ap, lhsT=qkT[k2][:, h * P:(h + 1) * P], rhs=sl[:, h * 256:(h + 1) * 256], start=False, stop=True)
                        return ins
                    reads = [R_sT[k2], R_qT[k2], R_v[g][hp], R_stbf[l]] + ([R_stloc[(g - 1) % 2]] if g > 0 else [])
                    pe_group(reads, b_o[hp], of)
                for hp in range(2):
                    S.op("act", [R_bank[b_o[hp]]], [R_on[k2]], lambda e, hp=hp: e.activation(
                        out=on32[k2][:, hp * 512:(hp + 1) * 512], in_=banks[b_o[hp]][:, :], func=AF.Copy))
                    bfree(b_o[hp])

            def U(g):
                if g < G - 1:
                    sl, Rsl = stloc[g % 2], R_stloc[g % 2]
                    S.op("act", [R_bank[b_S[0]]], [Rsl], lambda e: e.activation(out=sl[:, 0:512], in_=banks[b_S[0]][:, :], func=AF.Copy))
                    S.op("act", [R_bank[b_S[1]]], [Rsl], lambda e: e.activation(out=sl[:, 512:1024], in_=banks[b_S[1]][:, :], func=AF.Copy))
                else:
                    for hp in range(2):
                        S.op("dve", [R_bank[b_S[hp]], R_st32[l]], [R_st32[l]], lambda e, hp=hp: e.tensor_tensor(
                            out=st32[l][:, 2 * hp:2 * hp + 2, :], in0=banks[b_S[hp]][:, :].rearrange("p (h v) -> p h v", h=2),
                            in1=st32[l][:, 2 * hp:2 * hp + 2, :], op=ALU.add))
                    bfree(b_S[0])
                    bfree(b_S[1])
                    for h in range(H):
                        S.op("dve", [R_st32[l]], [R_st32[l]], lambda e, h=h: e.tensor_scalar(
                            out=st32[l][:, h, :], in0=st32[l][:, h, :], scalar1=GC4[h], scalar2=None, op0=ALU.mult))
                    S.op("act", [R_st32[l]], [R_stbf[l]], lambda e: e.activation(out=stbf[l][:], in_=st32[l][:].rearrange("p h v -> p (h v)"), func=AF.Copy))

            def Ns(g):
                k2 = g % 2
                for h in range(H):
                    S.op("dve", [R_on[k2]], [R_gn], lambda e, h=h: e.bn_stats(out=bnst[:, h, :], in_=on32[k2][:, h * 256:(h + 1) * 256]))
                for h in range(H):
                    S.op("dve", [R_gn], [R_gn], lambda e, h=h: e.bn_aggr(out=mv[k2][:, h, :], in_=bnst[:, h, :]))
                S.op("dve", [R_gn], [R_gn2[k2]], lambda e: e.tensor_scalar(out=gn_r[k2][:], in0=mv[k2][:, :, 1], scalar1=EPS, scalar2=None, op0=ALU.add))
                S.op("pool", [R_gn2[k2], R_cst], [R_gn2[k2]], lambda e: e.tensor_tensor(out=gn_r[k2][:], in0=gn_r[k2][:], in1=cst[:, C_MH:C_MH + 4], op=ALU.pow))

            def Nn(g):
                k2 = g % 2
                for h in range(H):
                    S.op("dve", [R_gn2[k2], R_on[k2]], [R_on[k2]], lambda e, h=h: e.tensor_scalar(
                        out=on32[k2][:, h * 256:(h + 1) * 256], in0=on32[k2][:, h * 256:(h + 1) * 256],
                        scalar1=mv[k2][:, h, 0:1], scalar2=gn_r[k2][:, h:h + 1], op0=ALU.subtract, op1=ALU.mult))
                S.op("pool", [R_on[k2], R_gg[g][0], R_gg[g][1]], [R_gated[k2]], lambda e: e.tensor_tensor(
                    out=gated[k2][:], in0=on32[k2][:], in1=gg_bf[:, g, :], op=ALU.mult))

            def E(g):
                k2 = g % 2
                for half in range(2):
                    b = balloc()
                    tp = banks[b][:].bitcast(BF16)

                    def tr2(e, half=half, tp=tp):
                        ins = None
                        for j in range(4):
                            c = half * 4 + j
                            ins = e.transpose(out=tp[:, j * P:(j + 1) * P], in_=gated[k2][:, c * P:(c + 1) * P], identity=ident[:])
                        return ins
                    pe_group([R_gated[k2], R_ident], b, tr2)
                    S.op("dve", [R_bank[b], R_par], [R_fm[16 + half * 4 + j][g] for j in range(4)],
                         lambda e, half=half, tp=tp: e.tensor_tensor(
                             out=fm[:, 16 + half * 4:16 + half * 4 + 4, g * P:(g + 1) * P],
                             in0=tp[:, 0:512].rearrange("p (j t) -> p j t", j=4),
                             in1=par[:, pr + half * 4:pr + half * 4 + 4].unsqueeze(2).to_broadcast([P, 4, P]),
                             op=ALU.mult))
                    bfree(b)

            def F(n=1):
                for _ in range(n):
                    if fillers is not None:
                        next(fillers, None)

            A(0); A(1)
            F(2)
            B(0); Dm(0); U(0); B(1)
            C(0); A(2); F(2)
            Dm(1); U(1); B(2); Ns(0)
            C(1); A(3); F(2)
            Dm(2); U(2); B(3); Nn(0); Ns(1)
            C(2); F(2)
            Dm(3); Nn(1); Ns(2)
            E(0)
            C(3); F(2)
            Nn(2); Ns(3)
            E(1)
            U(3)
            F(2)
            Nn(3)
            E(2)
            F(4)
            E(3)
            F(1)
            if fillers is not None:
                for _ in fillers:
                    pass

        e_state = {}

        def e_pre_steps(l, wi, c):
            slot = wget(wi + c)
            b_ga = balloc()
            mm_fm(b_ga, slot, 2, 0, 0)
            iA = salloc()
            S.op("act", [R_bank[b_ga]], [R_scr[iA]], lambda e: e.activation(out=scr[iA][:], in_=banks[b_ga][:, :], func=AF.Sigmoid))
            bfree(b_ga)
            yield
            b_yc = balloc()
            mm_fm(b_yc, slot, 0, 0, 8)
            S.op("dve", [R_bank[b_yc], R_scr[iA]], [R_scr[iA]], lambda e: e.tensor_tensor(out=scr[iA][:], in0=banks[b_yc][:, :], in1=scr[iA][:], op=ALU.mult))
            bfree(b_yc)
            yield
            b_gb = balloc()
            mm_fm(b_gb, slot, 3, 0, 0)
            iB = salloc()
            S.op("act", [R_bank[b_gb]], [R_scr[iB]], lambda e: e.activation(out=scr[iB][:], in_=banks[b_gb][:, :], func=AF.Sigmoid))
            bfree(b_gb)
            e_state[c] = (slot, iA, iB)
            yield

        def e_post(l, wi, c):
            if c not in e_state:
                for _ in e_pre_steps(l, wi, c):
                    pass
            slot, iA, iB = e_state.pop(c)
            b_yr = balloc()
            mm_fm(b_yr, slot, 1, 0, 16)
            S.op("dve", [R_bank[b_yr], R_scr[iB]], [R_scr[iB]], lambda e: e.tensor_tensor(out=scr[iB][:], in0=banks[b_yr][:, :], in1=scr[iB][:], op=ALU.mult))
            bfree(b_yr)
            S.op("pool", [R_scr[iA], R_scr[iB]], R_fm[24 + c], lambda e: e.tensor_tensor(
                out=fm[:, 24 + c, :], in0=scr[iA][:], in1=scr[iB][:], op=ALU.add))
            sfree(iA, iB)
            wrelease(wi + c)

        def stage_e(l, wi):
            for c in range(8):
                e_post(l, wi, c)

        def run_tail(main, norm, dep):
            if norm is None:
                for m in main:
                    m()
                return
            pre, tr = norm
            if dep:
                seq = [main[0], main[1], main[2], main[3], main[4], lambda: pre(0), main[5], lambda: pre(1), lambda: tr(0),
                       main[6], lambda: pre(2), lambda: tr(1), main[7], lambda: pre(3), lambda: tr(2), lambda: tr(3)]
            else:
                seq = [main[0], lambda: pre(0), main[1], lambda: pre(1), main[2], lambda: tr(0), main[3], lambda: pre(2),
                       main[4], lambda: tr(1), main[5], lambda: pre(3), main[6], lambda: tr(2), main[7], lambda: tr(3)]
            for f in seq:
                f()

        def stage_wo(xb, l, wi, norm):
            x_sb, R_x = x_sbs[xb], R_xs2[xb]
            slots = {}

            def M(cb, g):
                def f():
                    if g == 0:
                        slots[cb] = wget(wi + cb)
                    b = balloc()
                    mm_tm(b, [slots[cb]], 24, g, split=(cb == 0 and g < 2))
                    S.op("dve", [R_bank[b], R_x[g][cb]], [R_x[g][cb]], lambda e: e.tensor_tensor(
                        out=x_sb[:, g, cb * 512:(cb + 1) * 512], in0=banks[b][:, :], in1=x_sb[:, g, cb * 512:(cb + 1) * 512], op=ALU.add))
                    bfree(b)
                    if g == G - 1:
                        wrelease(wi + cb)
                return f
            run_tail([M(cb, g) for cb in range(2) for g in range(G)], norm, True)
            rms_stats2(xb)

        def stage_mlp(xb, l, wi, norm, dep, mid_hook=None):
            x_sb, R_x = x_sbs[xb], R_xs2[xb]
            for hf in range(2):
                base = wi + hf * 8
                for j in range(16):
                    blk = base + j // 4
                    slot = wget(blk)
                    b = balloc()
                    mm_fm(b, slot, j % 4, 0, 0, split=(j == 0))
                    i_r = salloc()
                    S.op("act", [R_bank[b]], [R_scr[i_r]], lambda e, b=b, i=i_r: e.activation(out=scr[i][:], in_=banks[b][:, :], func=AF.Relu))
                    bfree(b)
                    eng = "pool" if j % 2 == 0 else "dve"
                    S.op(eng, [R_scr[i_r]], R_fm[8 + j], lambda e, i=i_r, j=j: e.tensor_tensor(out=fm[:, 8 + j, :], in0=scr[i][:], in1=scr[i][:], op=ALU.mult))
                    sfree(i_r)
                    if j % 4 == 3:
                        wrelease(blk)
                slots = {}

                def M(cb, g, base=base, slots=slots):
                    def f():
                        if g == 0:
                            slots[cb] = (wget(base + 4 + cb * 2), wget(base + 4 + cb * 2 + 1))
                        b = balloc()
                        mm_tm(b, list(slots[cb]), 8, g, split=(cb == 0 and g < 2))
                        S.op("dve", [R_bank[b], R_x[g][cb], R_ssq2], [R_x[g][cb]], lambda e: e.scalar_tensor_tensor(
                            out=x_sb[:, g, cb * 512:(cb + 1) * 512], in0=banks[b][:, :], scalar=rstd2[:, g:g + 1],
                            in1=x_sb[:, g, cb * 512:(cb + 1) * 512], op0=ALU.mult, op1=ALU.add))
                        bfree(b)
                        if g == G - 1:
                            wrelease(base + 4 + cb * 2)
                            wrelease(base + 4 + cb * 2 + 1)
                    return f
                main = [M(cb, g) for cb in range(2) for g in range(G)]
                if hf == 0 and mid_hook is not None:
                    main[0]()
                    main[1]()
                    mid_hook()
                    main = main[2:]
                    for m in main:
                        m()
                else:
                    run_tail(main, norm if hf == 1 else None, dep)

        def stage_final(it):
            xb = it % 2
            x_sb, R_x = x_sbs[xb], R_xs2[xb]
            for g in range(G):
                k = g % 2
                rms_stats(xb, g)
                S.op("act", R_x[g] + [R_ssq], [R_on[k]], lambda e, g=g, k=k: e.activation(
                    out=on32[k][:], in_=x_sb[:, g, :], func=AF.Copy, scale=rstd[:, g:g + 1]))
                S.op("dve", [R_on[k], R_par], [R_on[k]], lambda e, k=k: e.tensor_tensor(out=on32[k][:], in0=on32[k][:], in1=par[:, P_NF:P_NF + D], op=ALU.mult))
                r0 = it * T + g * P
                S.dma("sp", "out%d" % k, [R_on[k]], [R_out], [lambda e, k=k, r0=r0: e.dma_start(out=out_d[r0:r0 + P, :], in_=on32[k][:])])

        def load_x(it):
            xb = it % 2
            S.dma("sp", "xld", [], [r for rr in R_xs2[xb] for r in rr], [
                lambda e, g=g: e.dma_start(out=x_sbs[xb][:, g, :], in_=x_d[it * T + g * P: it * T + (g + 1) * P, :]) for g in range(G)])

        wi = 0
        for it in range(ntiles):
            xb = it % 2
            if it == 0:
                load_x(0)
                tables_a(0)
                tables_b(0)
                norm_full(xb, 0 * PL + 0)
            for l in range(nlayers):
                for c in range(4):
                    conv_chunk(l, c, wi + B_CONVA)
                if l == 0 and it + 1 < ntiles:
                    load_x(it + 1)
                stage_b2(l, wi + B_B2)
                for c in range(4, 8):
                    conv_chunk(l, c, wi + B_CONVB)
                if l == nlayers - 1 and it + 1 < ntiles:
                    tables_a(it + 1)

                def fill_gen(l=l, wi=wi):
                    for g in range(G):
                        for gb_ in range(2):
                            gproj_unit(l, wi + B_G, gb_, g)
                            yield
                    for c in range(3):
                        for _ in e_pre_steps(l, wi + B_E, c):
                            yield
                stage_ret(l, fill_gen())
                stage_e(l, wi + B_E)
                stage_wo(xb, l, wi + B_WO, make_norm(xb, l * PL + 16, raw=True))
                if l < nlayers - 1:
                    stage_mlp(xb, l, wi + B_MLP, make_norm(xb, (l + 1) * PL + 0), True)
                elif it + 1 < ntiles:
                    stage_mlp(xb, l, wi + B_MLP, make_norm((it + 1) % 2, 0), False, mid_hook=lambda: tables_b(it + 1))
                else:
                    stage_mlp(xb, l, wi + B_MLP, None, True)
                wi += NBLK
            stage_final(it)

        ensure_cast(ncast_total - 1)
        if dbg_d is not None:
            pass
        for key in list(S.dlast.keys()):
            S._wait("sp", S.dlast[key])
        build.stats = dict(nops=dict(S.nops), nwaits=S.nwaits)
    return nc


def make_consts():
    c = np.zeros((P, NCONST), np.float64)
    c[:, C_ID:C_ID + 128] = np.eye(128)
    j = np.arange(128, dtype=np.float64)
    for h in range(H):
        gam = 1.0 - 2.0 ** (-5.0 - h)
        gC = gam ** 128.0
        for g in range(G):
            c[:, C_DQ + h * 4 + g] = gam ** (j + 1.0) * gC ** g * (128.0 ** -0.5)
            c[:, C_DK + h * 4 + g] = gam ** (127.0 - j) * gC ** (-(g + 1.0))
        m = (j[None, :] >= j[:, None]).astype(np.float64)
        c[:, C_MASK + h * 128:C_MASK + (h + 1) * 128] = m
    c[0, C_INVF:C_INVF + 64] = 10000.0 ** (-np.arange(64, dtype=np.float64) / 64.0)
    c[:, C_MH:C_MH + 4] = -0.5
    return c.astype(np.float32)


def make_params(norm_mix, conv_w, ret_norm, norm_mlp, norm_final):
    p = np.zeros((P, NPAR), np.float32)
    for l in range(L):
        o = l * PL
        p[:, o + 0:o + 8] = np.asarray(norm_mix[l]).reshape(8, P).T
        p[:, o + 8:o + 16] = np.asarray(ret_norm[l]).reshape(8, P).T
        p[:, o + 16:o + 24] = np.asarray(norm_mlp[l]).reshape(8, P).T
        p[:, o + 24:o + 48] = np.asarray(conv_w[l]).reshape(3, 8, P).transpose(2, 1, 0).reshape(P, 24)
    p[:, P_NF:P_NF + D] = np.asarray(norm_final)[None, :]
    return p


_NC_CACHE = {}


def kernel(x, positions, norm_mix, w_in, conv_w, w_conv_out, ret_norm, w_ret_out, w_o, norm_mlp, w_up, w_down, norm_final):
    x = np.asarray(x, np.float32)
    B, S_, _ = x.shape
    ntiles = S_ // T
    key = (ntiles,)
    if key not in _NC_CACHE:
        _NC_CACHE[key] = build(ntiles=ntiles)
    nc = _NC_CACHE[key]
    cst = make_consts()
    par = make_params(norm_mix, conv_w, ret_norm, norm_mlp, norm_final)
    shared = {
        "cst": cst, "par": par,
        "w_in": np.ascontiguousarray(w_in, np.float32), "w_conv_out": np.ascontiguousarray(w_conv_out, np.float32),
        "w_ret_out": np.ascontiguousarray(w_ret_out, np.float32), "w_o": np.ascontiguousarray(w_o, np.float32),
        "w_up": np.ascontiguousarray(w_up, np.float32), "w_down": np.ascontiguousarray(w_down, np.float32),
    }
    pos = np.asarray(positions, np.int32)
    in_maps = []
    for b in range(B):
        m = dict(shared)
        m["x"] = np.ascontiguousarray(x[b])
        m["pos"] = np.ascontiguousarray(pos[b][None, :])
        in_maps.append(m)
    res = run_bass_kernel_spmd(nc, in_maps, core_ids=list(range(B)))
    return np.stack([np.asarray(r["out"]) for r in res.results], axis=0).astype(np.float32)
```

```python
import numpy as np
from contextlib import ExitStack
import concourse.bass as bass
import concourse.mybir as mybir
from concourse.bass_utils import run_bass_kernel_spmd

F32 = mybir.dt.float32
BF16 = mybir.dt.bfloat16
I32 = mybir.dt.int32
AF = mybir.ActivationFunctionType
ALU = mybir.AluOpType
PI = float(np.pi)

P = 128
T = 512
G = 4
D = 1024
L = 2
H = 4
EPS = 1e-6
NB = 4
NBLK = 38
CAST_AHEAD = 10
N_CAST_SEM = 12

O_CB, O_CC, O_CU, O_Q, O_K, O_V, O_G, O_GA, O_GB = 0, 1024, 2048, 3072, 3584, 4096, 5120, 6144, 7168

C_ID = 0
C_MASK = 128
C_DQ = 640
C_DK = 656
C_INVF = 672
C_MH = 736
NCONST = 740
GC4 = [float((1.0 - 2.0 ** (-5.0 - h)) ** 512.0) for h in range(4)]
PL = 48
P_NF = L * PL
NPAR = P_NF + 1024


class Tok:
    __slots__ = ("sem", "val")

    def __init__(self, sem, val):
        self.sem = sem
        self.val = val


class Res:
    __slots__ = ("name", "w", "r", "excl")

    def __init__(self, name="", excl=False):
        self.name = name
        self.w = None
        self.r = {}
        self.excl = excl


class Sched:
    ENG = ("pe", "act", "dve", "pool", "sp")

    def __init__(self, nc, stack):
        self.nc = nc
        self.eng = {"pe": nc.tensor, "act": nc.scalar, "dve": nc.vector, "pool": nc.gpsimd, "sp": nc.sync}
        self.sem = {e: stack.enter_context(nc.semaphore("s_" + e)) for e in self.ENG}
        self.cnt = {e: 0 for e in self.ENG}
        self.known = {e: {} for e in self.ENG}
        self.dsem, self.dcnt, self.dlast = {}, {}, {}
        self.stack = stack
        self.nwaits = 0
        self.nops = {e: 0 for e in self.ENG}

    def _wait(self, e, tok):
        if tok is None:
            return
        k = self.known[e]
        key = tok.sem.num
        if k.get(key, 0) >= tok.val:
            return
        self.eng[e].wait_ge(tok.sem, tok.val)
        self.nwaits += 1
        k[key] = tok.val

    def _deps(self, e, reads, writes):
        own = self.sem[e]
        for r in reads:
            self._wait(e, r.w)
            if r.excl:
                for t in list(r.r.values()):
                    if t.sem is not own:
                        self._wait(e, t)
        for w in writes:
            self._wait(e, w.w)
            for t in list(w.r.values()):
                self._wait(e, t)

    def _commit(self, tok, reads, writes):
        key = tok.sem.num
        for r in reads:
            old = r.r.get(key)
            if old is None or old.val < tok.val:
                r.r[key] = tok
        for w in writes:
            w.w = tok
            w.r = {}

    def op(self, e, reads, writes, fn):
        self._deps(e, reads, writes)
        ins = fn(self.eng[e])
        self.cnt[e] += 1
        self.nops[e] += 1
        ins.then_inc(self.sem[e], 1)
        tok = Tok(self.sem[e], self.cnt[e])
        self._commit(tok, reads, writes)
        return tok

    def dma(self, q, semkey, reads, writes, fns):
        if semkey not in self.dsem:
            self.dsem[semkey] = self.stack.enter_context(self.nc.semaphore("d_" + str(semkey)))
            self.dcnt[semkey] = 0
            self.dlast[semkey] = None
        sem = self.dsem[semkey]
        self._wait(q, self.dlast[semkey])
        self._deps(q, reads, writes)
        for fn in fns:
            fn(self.eng[q]).then_inc(sem, 16)
            self.dcnt[semkey] += 16
        tok = Tok(sem, self.dcnt[semkey])
        self.dlast[semkey] = tok
        self._commit(tok, reads, writes)
        return tok


def layer_blocks():
    blocks = []
    chunks = []
    for c in range(8):
        chunks += [("w_in", 0, O_CC + c * 128, 128), ("w_in", 0, O_CU + c * 128, 128), ("w_in", 0, O_CB + c * 128, 128)]
    for j in range(3):
        blocks.append(chunks[4 * j:4 * j + 4])
    for off in (O_Q, O_K, O_V, O_V + 512):
        blocks.append([("w_in", 0, off, 512)])
    for j in range(3, 6):
        blocks.append(chunks[4 * j:4 * j + 4])
    for off in (O_G, O_G + 512):
        blocks.append([("w_in", 0, off, 512)])
    for c in range(8):
        blocks.append([("w_conv_out", 0, c * 128, 128), ("w_ret_out", 0, c * 128, 128),
                       ("w_in", 0, O_GA + c * 128, 128), ("w_in", 0, O_GB + c * 128, 128)])
    for cb in range(2):
        blocks.append([("w_o", 0, cb * 512, 512)])
    for hf in range(2):
        for j in range(4):
            blocks.append([("w_up", 0, (hf * 4 + j) * 512, 512)])
        for cb in range(2):
            for kg in range(2):
                blocks.append([("w_down", (hf * 16 + kg * 8) * 128, cb * 512, 512)])
    assert len(blocks) == NBLK
    return blocks


B_CONVA, B_B2, B_CONVB, B_G, B_E, B_WO, B_MLP = 0, 3, 7, 10, 12, 20, 22


def build(ntiles=8, nlayers=L, dbg=None):
    S_TOK = ntiles * T
    nc = bass.Bass("TRN2", target_bir_lowering=False)
    x_d = nc.dram_tensor("x", [S_TOK, D], F32, kind="ExternalInput").ap()
    pos_d = nc.dram_tensor("pos", [1, S_TOK], I32, kind="ExternalInput").ap()
    cst_d = nc.dram_tensor("cst", [P, NCONST], F32, kind="ExternalInput").ap()
    par_d = nc.dram_tensor("par", [P, NPAR], F32, kind="ExternalInput").ap()
    wd = {
        "w_in": nc.dram_tensor("w_in", [L, D, 8192], F32, kind="ExternalInput").ap(),
        "w_conv_out": nc.dram_tensor("w_conv_out", [L, D, D], F32, kind="ExternalInput").ap(),
        "w_ret_out": nc.dram_tensor("w_ret_out", [L, D, D], F32, kind="ExternalInput").ap(),
        "w_o": nc.dram_tensor("w_o", [L, D, D], F32, kind="ExternalInput").ap(),
        "w_up": nc.dram_tensor("w_up", [L, D, 4096], F32, kind="ExternalInput").ap(),
        "w_down": nc.dram_tensor("w_down", [L, 4096, D], F32, kind="ExternalInput").ap(),
    }
    out_d = nc.dram_tensor("out", [S_TOK, D], F32, kind="ExternalOutput").ap()
    wbf_d = nc.dram_tensor("wbf", [L * NBLK * P, 4096], BF16).ap()
    dbg_d = None
    if dbg is not None:
        dbg_d = nc.dram_tensor("dbg", [P, dbg], F32, kind="ExternalOutput").ap()

    LB = layer_blocks()

    with ExitStack() as st:
        S = Sched(nc, st)
        sb = lambda name, shape, dt: st.enter_context(nc.sbuf_tensor("sb_" + name, shape, dt))

        cst = sb("cst", [P, NCONST], F32)
        par = sb("par", [P, NPAR], F32)
        ident = sb("ident", [P, P], BF16)
        fm = sb("fm", [P, 32, T], BF16)
        x_sbs = [sb("x_sb%d" % i, [P, G, D], F32) for i in range(2)]
        xs_bf = [sb("xs%d" % i, [P, D], BF16) for i in range(2)]
        wring = [sb("wr%d" % i, [P, 4096], BF16) for i in range(NB)]
        scr = [sb("scr%d" % i, [P, T], F32) for i in range(6)]
        zbuf = [sb("zb%d" % i, [P, T + 2], F32) for i in range(2)]
        halo = sb("halo", [P, L, 8, 2], F32)
        qr = sb("qr", [P, G, 512], BF16)
        kr = sb("kr", [P, G, 512], BF16)
        v_bf = sb("v_bf", [P, G, D], BF16)
        gg_bf = sb("gg_bf", [P, G, D], BF16)
        tabs = {k: sb("tab_" + k, [P, G, H, 64], F32) for k in ("cq", "sq", "ck", "sk")}
        cosb = sb("cosb", [P, G, 64], F32)
        sinb = sb("sinb", [P, G, 64], F32)
        posi = sb("posi", [1, T], I32)
        posf = sb("posf", [1, T], F32)
        qkT = [sb("qkT%d" % i, [P, 1024], BF16) for i in range(2)]
        sT = [sb("sT%d" % i, [P, 512], BF16) for i in range(2)]
        stloc = [sb("stloc%d" % i, [P, D], BF16) for i in range(2)]
        st32 = [sb("st32_%d" % l, [P, H, 256], F32) for l in range(L)]
        stbf = [sb("stbf_%d" % l, [P, D], BF16) for l in range(L)]
        on32 = [sb("on32_%d" % i, [P, D], F32) for i in range(2)]
        gated = [sb("gated%d" % i, [P, D], BF16) for i in range(2)]
        bnst = sb("bnst", [P, H, 6], F32)
        mv = [sb("mv%d" % i, [P, H, 2], F32) for i in range(2)]
        gn_r = [sb("gn_r%d" % i, [P, H], F32) for i in range(2)]
        ssq = sb("ssq", [P, G], F32)
        rstd = sb("rstd", [P, G], F32)
        ssq2 = sb("ssq2", [P, G], F32)
        rstd2 = sb("rstd2", [P, G], F32)
        banks = [st.enter_context(nc.psum_tensor("bk%d" % i, [P, 512], F32)) for i in range(8)]

        R_cst, R_par, R_ident = Res("cst"), Res("par"), Res("ident")
        R_fm = [[Res("fm%d_%d" % (c, g)) for g in range(G)] for c in range(32)]
        R_xs2 = [[[Res("x%d_%d_%d" % (i, g, cb)) for cb in range(2)] for g in range(G)] for i in range(2)]
        R_xs = [Res("xs0"), Res("xs1")]
        R_slot = [Res("slot%d" % i) for i in range(NB)]
        R_wbf = [Res("wbf%d" % i) for i in range(L * NBLK)]
        R_scr = [Res("scr%d" % i) for i in range(6)]
        R_zb = [Res("zb0"), Res("zb1")]
        R_halo = [[Res("halo%d_%d" % (l, c)) for c in range(8)] for l in range(L)]
        R_qr = [Res("qr%d" % g) for g in range(G)]
        R_kr = [Res("kr%d" % g) for g in range(G)]
        R_v = [[Res("v%d_%d" % (g, b)) for b in range(2)] for g in range(G)]
        R_gg = [[Res("gg%d_%d" % (g, b)) for b in range(2)] for g in range(G)]
        R_tab = Res("tabs")
        R_trig = Res("trig")
        R_posi, R_posf = Res("posi"), Res("posf")
        R_qT, R_kT, R_sT = [Res(), Res()], [Res(), Res()], [Res(), Res()]
        R_st32 = [Res("st32_0"), Res("st32_1")]
        R_stbf = [Res("stbf0"), Res("stbf1")]
        R_stloc = [Res("stloc0"), Res("stloc1")]
        R_on = [Res("on0"), Res("on1")]
        R_gated = [Res("gated0"), Res("gated1")]
        R_gn = Res("gnstats")
        R_gn2 = [Res("gnrb0"), Res("gnrb1")]
        R_ssq = Res("ssq")
        R_ssq2 = Res("ssq2")
        R_bank = [Res("bank%d" % i, excl=True) for i in range(8)]
        R_out = Res("out")
        R_dbg = Res("dbg")

        free_banks = list(range(8))

        def balloc():
            assert free_banks, "out of PSUM banks"
            return free_banks.pop(0)

        def bfree(b):
            free_banks.append(b)

        free_scr = list(range(len(scr)))

        def salloc():
            assert free_scr, "out of scratch buffers"
            return free_scr.pop(0)

        def sfree(*idx):
            for i in idx:
                free_scr.append(i)

        alt = [0]

        def alt_eng():
            alt[0] ^= 1
            return "act" if alt[0] else "dve"

        total_stream = ntiles * nlayers * NBLK
        wst = {"next_load": 0, "next_cast": 0}
        ncast_total = nlayers * NBLK

        def emit_cast(sid):
            l, j = divmod(sid, NBLK)
            fns = []
            off = 0
            for (name, row0, col0, ncols) in LB[j]:
                src = wd[name][l, row0:row0 + 8 * P, col0:col0 + ncols].rearrange("(kc p) c -> p kc c", p=P)
                dst = wbf_d[sid * P:(sid + 1) * P, :].rearrange("p (kc c) -> p kc c", kc=8)[:, :, off:off + ncols]
                fns.append(lambda e, src=src, dst=dst: e.dma_start(out=dst, in_=src))
                off += ncols
            S.dma("pool", "cast%d" % (sid % N_CAST_SEM), [], [R_wbf[sid]], fns)

        def ensure_cast(upto):
            while wst["next_cast"] <= min(upto, ncast_total - 1):
                emit_cast(wst["next_cast"])
                wst["next_cast"] += 1

        def stream_sid(i):
            return ((i // NBLK) % nlayers) * NBLK + (i % NBLK)

        def emit_load(i):
            sid = stream_sid(i)
            ensure_cast(sid + CAST_AHEAD)
            slot = i % NB
            S.dma("sp", "slot%d" % slot, [R_wbf[sid]], [R_slot[slot]],
                  [lambda e, slot=slot, sid=sid: e.dma_start(out=wring[slot][:], in_=wbf_d[sid * P:(sid + 1) * P, :])])

        def wget(i):
            while wst["next_load"] <= i:
                assert wst["next_load"] < NB, "weight block requested before its slot was released"
                emit_load(wst["next_load"])
                wst["next_load"] += 1
            return i % NB

        def wrelease(i):
            nxt = i + NB
            assert wst["next_load"] == nxt, (wst["next_load"], nxt)
            if nxt < total_stream:
                emit_load(nxt)
            wst["next_load"] = nxt + 1

        S.dma("sp", "c0", [], [R_cst], [lambda e: e.dma_start(out=cst[:], in_=cst_d)])
        S.dma("sp", "c1", [], [R_par], [lambda e: e.dma_start(out=par[:], in_=par_d)])
        S.op("dve", [R_cst], [R_ident], lambda e: e.tensor_copy(out=ident[:], in_=cst[:, C_ID:C_ID + 128]))
        for l in range(L):
            S.op("dve", [], [R_st32[l]], lambda e, l=l: e.memset(st32[l][:], 0.0))
            S.op("dve", [], [R_stbf[l]], lambda e, l=l: e.memset(stbf[l][:], 0.0))
        S.op("dve", [], [r for rr in R_halo for r in rr], lambda e: e.memset(halo[:], 0.0))
        for i in range(NB):
            wget(i)

        maskT = cst[:, C_MASK:C_MASK + 512]
        mhalf = cst[:, C_MH:C_MH + 1]

        def pe_group(reads, bank, fn):
            return S.op("pe", reads, [R_bank[bank]], fn)

        def rms_stats(xb, g):
            k = g % 2
            x_sb, R_x = x_sbs[xb], R_xs2[xb]
            S.op("act", R_x[g], [R_xs[k], R_ssq], lambda e: e.activation(
                out=xs_bf[k][:], in_=x_sb[:, g, :], func=AF.Square, accum_out=ssq[:, g:g + 1]))
            S.op("dve", [R_ssq], [R_ssq], lambda e: e.tensor_scalar(
                out=rstd[:, g:g + 1], in0=ssq[:, g:g + 1], scalar1=1.0 / D, scalar2=EPS, op0=ALU.mult, op1=ALU.add))
            S.op("pool", [R_ssq, R_cst], [R_ssq], lambda e: e.tensor_tensor(
                out=rstd[:, g:g + 1], in0=rstd[:, g:g + 1], in1=mhalf, op=ALU.pow))

        def rms_stats2(xb):
            x_sb, R_x = x_sbs[xb], R_xs2[xb]
            for g in range(G):
                k = g % 2
                S.op("act", R_x[g], [R_gated[k], R_ssq2], lambda e, g=g, k=k: e.activation(
                    out=gated[k][:], in_=x_sb[:, g, :], func=AF.Square, accum_out=ssq2[:, g:g + 1]))
            S.op("dve", [R_ssq2], [R_ssq2], lambda e: e.tensor_scalar(
                out=rstd2[:], in0=ssq2[:], scalar1=1.0 / D, scalar2=EPS, op0=ALU.mult, op1=ALU.add))
            S.op("dve", [R_ssq2], [R_ssq2], lambda e: e.reciprocal(out=rstd2[:], in_=rstd2[:]))

        def make_norm(xb, gain_off, raw=False):
            x_sb, R_x = x_sbs[xb], R_xs2[xb]

            def pre(g):
                k = g % 2
                if raw:
                    S.op("act", R_x[g], [R_xs[k]], lambda e: e.activation(out=xs_bf[k][:], in_=x_sb[:, g, :], func=AF.Copy))
                    return
                rms_stats(xb, g)
                S.op("act", R_x[g] + [R_ssq], [R_xs[k]], lambda e: e.activation(
                    out=xs_bf[k][:], in_=x_sb[:, g, :], func=AF.Copy, scale=rstd[:, g:g + 1]))

            def tr(g):
                k = g % 2
                for half in range(2):
                    b = balloc()
                    tp = banks[b][:].bitcast(BF16)

                    def trf(e, half=half, tp=tp):
                        ins = None
                        for j in range(4):
                            c = half * 4 + j
                            ins = e.transpose(out=tp[:, j * P:(j + 1) * P], in_=xs_bf[k][:, c * P:(c + 1) * P], identity=ident[:])
                        return ins
                    pe_group([R_xs[k], R_ident], b, trf)
                    S.op("dve", [R_bank[b], R_par], [R_fm[half * 4 + j][g] for j in range(4)],
                         lambda e, half=half, tp=tp: e.tensor_tensor(
                             out=fm[:, half * 4:half * 4 + 4, g * P:(g + 1) * P],
                             in0=tp[:, 0:512].rearrange("p (j t) -> p j t", j=4),
                             in1=par[:, gain_off + half * 4:gain_off + half * 4 + 4].unsqueeze(2).to_broadcast([P, 4, P]),
                             op=ALU.mult))
                    bfree(b)
            return pre, tr

        def norm_full(xb, gain_off):
            pre, tr = make_norm(xb, gain_off)
            for g in range(G):
                pre(g)
                tr(g)

        def mm_fm(bank, slot, sub, kbase_chunk, src_chunk0, split=False):
            parts = [(0, 4), (4, 8)] if split else [(0, 8)]
            tok = None
            for (lo, hi) in parts:
                def fn(e, lo=lo, hi=hi):
                    ins = None
                    for kc in range(lo, hi):
                        ins = e.matmul(banks[bank][:, :], lhsT=wring[slot][:, kc * 512 + sub * P: kc * 512 + (sub + 1) * P],
                                       rhs=fm[:, src_chunk0 + kc, :], start=(kc == 0), stop=(kc == 7))
                    return ins
                reads = [R_slot[slot]] + [R_fm[src_chunk0 + kc][g] for kc in range(lo, hi) for g in range(G)]
                tok = pe_group(reads, bank, fn)
            return tok

        def mm_tm(bank, slots, src_chunk0, g, split=False, part=None):
            n = len(slots) * 8
            parts = [(0, n // 2), (n // 2, n)] if split else [(0, n)]
            if part is not None:
                parts = [(0, n // 2)] if part == 0 else [(n // 2, n)]
            tok = None
            for (lo, hi) in parts:
                def fn(e, lo=lo, hi=hi):
                    ins = None
                    for i in range(lo, hi):
                        si, kc = divmod(i, 8)
                        ins = e.matmul(banks[bank][:, :], lhsT=fm[:, src_chunk0 + i, g * P:(g + 1) * P],
                                       rhs=wring[slots[si]][:, kc * 512:(kc + 1) * 512], start=(i == 0), stop=(i == n - 1))
                    return ins
                reads = [R_slot[slots[i // 8]] for i in range(lo, hi, 8)] + [R_fm[src_chunk0 + i][g] for i in range(lo, hi)]
                tok = pe_group(reads, bank, fn)
            return tok

        def tables_a(it):
            S.dma("sp", "pos", [], [R_posi], [lambda e: e.dma_start(out=posi[:], in_=pos_d[:, it * T:(it + 1) * T])])
            S.op("dve", [R_posi], [R_posf], lambda e: e.tensor_copy(out=posf[:], in_=posi[:]))

        def tables_b(it):
            b = balloc()

            def fn(e):
                ins = None
                for g in range(G):
                    ins = e.matmul(banks[b][:, g * 64:(g + 1) * 64], lhsT=posf[0:1, g * P:(g + 1) * P],
                                   rhs=cst[0:1, C_INVF:C_INVF + 64], start=True, stop=True)
                return ins
            pe_group([R_posf, R_cst], b, fn)
            ia, ib, ic = salloc(), salloc(), salloc()
            Ra, Rb, Rc = R_scr[ia], R_scr[ib], R_scr[ic]
            angf = scr[ia][:, 0:256]
            angm = scr[ia][:, 256:512]
            angr = scr[ib][:, :].rearrange("p (t n) -> p t n", t=2)
            angk = scr[ic][:, 0:256].bitcast(I32)
            ang = banks[b][:, 0:G * 64]
            S.op("dve", [R_bank[b]], [Ra], lambda e: e.tensor_copy(out=angf, in_=ang))
            bfree(b)
            S.op("dve", [Ra], [Rc], lambda e: e.tensor_scalar(out=angk, in0=angf, scalar1=1.0 / (2 * PI), scalar2=None, op0=ALU.mult))
            S.op("dve", [Rc], [Ra], lambda e: e.tensor_copy(out=angm, in_=angk))
            S.op("dve", [Ra], [Rb], lambda e: e.scalar_tensor_tensor(out=angr[:, 0, :], in0=angm, scalar=-2 * PI, in1=angf, op0=ALU.mult, op1=ALU.add))
            S.op("dve", [Rb], [Ra], lambda e: e.tensor_scalar(out=angm, in0=angr[:, 0, :], scalar1=PI, scalar2=-2 * PI, op0=ALU.is_gt, op1=ALU.mult))
            S.op("dve", [Ra, Rb], [Rb], lambda e: e.tensor_tensor(out=angr[:, 0, :], in0=angr[:, 0, :], in1=angm, op=ALU.add))
            S.op("dve", [Rb], [Rb], lambda e: e.tensor_scalar(out=angr[:, 1, :], in0=angr[:, 0, :], scalar1=0.5 * PI, scalar2=None, op0=ALU.add))
            S.op("dve", [Rb], [Ra], lambda e: e.tensor_scalar(out=angm, in0=angr[:, 1, :], scalar1=PI, scalar2=-2 * PI, op0=ALU.is_gt, op1=ALU.mult))
            S.op("dve", [Ra, Rb], [Rb], lambda e: e.tensor_tensor(out=angr[:, 1, :], in0=angr[:, 1, :], in1=angm, op=ALU.add))
            S.op("dve", [Rb], [Rb], lambda e: e.tensor_scalar(out=angr, in0=angr, scalar1=-PI, scalar2=PI, op0=ALU.max, op1=ALU.min))
            S.op("act", [Rb], [R_trig], lambda e: e.activation(out=sinb[:].rearrange("p g i -> p (g i)"), in_=angr[:, 0, :], func=AF.Sin))
            S.op("act", [Rb], [R_trig], lambda e: e.activation(out=cosb[:].rearrange("p g i -> p (g i)"), in_=angr[:, 1, :], func=AF.Sin))
            sfree(ia, ib, ic)
            for h in range(H):
                for (tn, base, col) in (("cq", cosb, C_DQ), ("sq", sinb, C_DQ), ("ck", cosb, C_DK), ("sk", sinb, C_DK)):
                    S.op("dve", [R_trig, R_cst], [R_tab], lambda e, tn=tn, base=base, col=col, h=h: e.tensor_tensor(
                        out=tabs[tn][:, :, h, :], in0=base[:],
                        in1=cst[:, col + h * 4:col + h * 4 + 4].unsqueeze(2).to_broadcast([P, G, 64]), op=ALU.mult))

        def conv_chunk(l, c, wi):
            pw = l * PL + 24
            if True:
                sl = {}
                for kind in range(3):
                    n = 3 * (c % 4) + kind
                    sl[kind] = (wget(wi + n // 4), n % 4)
                b_cc = balloc()
                mm_fm(b_cc, sl[0][0], sl[0][1], 0, 0, split=(c == 0))
                i_cc = salloc()
                S.op("act", [R_bank[b_cc]], [R_scr[i_cc]], lambda e, b=b_cc, i=i_cc: e.activation(out=scr[i][:], in_=banks[b][:, :], func=AF.Copy))
                bfree(b_cc)
                b_cu = balloc()
                mm_fm(b_cu, sl[1][0], sl[1][1], 0, 0)
                zk = c % 2
                S.op("dve", [R_bank[b_cu], R_scr[i_cc]], [R_zb[zk]], lambda e, b=b_cu, i=i_cc, zk=zk: e.tensor_tensor(
                    out=zbuf[zk][:, 2:T + 2], in0=banks[b][:, :], in1=scr[i][:], op=ALU.mult))
                bfree(b_cu)
                sfree(i_cc)
                S.op("pool", [R_halo[l][c]], [R_zb[zk]], lambda e, zk=zk, c=c: e.tensor_copy(out=zbuf[zk][:, 0:2], in_=halo[:, l, c, :]))
                i_acc = salloc()
                S.op("dve", [R_zb[zk], R_par], [R_scr[i_acc]], lambda e, zk=zk, i=i_acc, c=c: e.tensor_scalar(
                    out=scr[i][:], in0=zbuf[zk][:, 2:T + 2], scalar1=par[:, pw + c * 3 + 2:pw + c * 3 + 3], scalar2=None, op0=ALU.mult))
                S.op("dve", [R_zb[zk], R_par, R_scr[i_acc]], [R_scr[i_acc]], lambda e, zk=zk, i=i_acc, c=c: e.scalar_tensor_tensor(
                    out=scr[i][:], in0=zbuf[zk][:, 1:T + 1], scalar=par[:, pw + c * 3 + 1:pw + c * 3 + 2], in1=scr[i][:], op0=ALU.mult, op1=ALU.add))
                S.op("dve", [R_zb[zk], R_par, R_scr[i_acc]], [R_scr[i_acc]], lambda e, zk=zk, i=i_acc, c=c: e.scalar_tensor_tensor(
                    out=scr[i][:], in0=zbuf[zk][:, 0:T], scalar=par[:, pw + c * 3:pw + c * 3 + 1], in1=scr[i][:], op0=ALU.mult, op1=ALU.add))
                S.op("pool", [R_zb[zk]], [R_halo[l][c]], lambda e, zk=zk, c=c: e.tensor_copy(out=halo[:, l, c, :], in_=zbuf[zk][:, T:T + 2]))
                b_cb = balloc()
                mm_fm(b_cb, sl[2][0], sl[2][1], 0, 0)
                S.op("dve", [R_bank[b_cb], R_scr[i_acc]], R_fm[8 + c], lambda e, b=b_cb, i=i_acc, c=c: e.tensor_tensor(
                    out=fm[:, 8 + c, :], in0=banks[b][:, :], in1=scr[i][:], op=ALU.mult))
                bfree(b_cb)
                sfree(i_acc)
                for kind in range(3):
                    n = 3 * (c % 4) + kind
                    if n % 4 == 3:
                        wrelease(wi + n // 4)

        def rotary(bank, g, dst, R_dst, tc, ts):
            src = banks[bank][:, :].rearrange("p (h t i) -> p h t i", h=H, t=2)
            t1 = src[:, :, 0, :]
            t2 = src[:, :, 1, :]
            out4 = dst[:, g, :].rearrange("p (h t i) -> p h t i", h=H, t=2)
            ia, ib = salloc(), salloc()
            a = scr[ia][:, 0:256].rearrange("p (h i) -> p h i", h=H)
            a2 = scr[ia][:, 256:512].rearrange("p (h i) -> p h i", h=H)
            bb = scr[ib][:, 0:256].rearrange("p (h i) -> p h i", h=H)
            b2 = scr[ib][:, 256:512].rearrange("p (h i) -> p h i", h=H)
            cosT = tabs[tc][:, g, :, :]
            sinT = tabs[ts][:, g, :, :]
            S.op("dve", [R_bank[bank], R_tab], [R_scr[ia]], lambda e: e.tensor_tensor(out=a, in0=t1, in1=cosT, op=ALU.mult))
            S.op("dve", [R_bank[bank], R_tab], [R_scr[ib]], lambda e: e.tensor_tensor(out=bb, in0=t2, in1=sinT, op=ALU.mult))
            S.op("dve", [R_bank[bank], R_tab], [R_scr[ia]], lambda e: e.tensor_tensor(out=a2, in0=t1, in1=sinT, op=ALU.mult))
            S.op("dve", [R_bank[bank], R_tab], [R_scr[ib]], lambda e: e.tensor_tensor(out=b2, in0=t2, in1=cosT, op=ALU.mult))
            S.op("pool", [R_scr[ia], R_scr[ib]], [R_dst], lambda e: e.tensor_tensor(out=out4[:, :, 0, :], in0=a, in1=bb, op=ALU.subtract))
            S.op("pool", [R_scr[ia], R_scr[ib]], [R_dst], lambda e: e.tensor_tensor(out=out4[:, :, 1, :], in0=a2, in1=b2, op=ALU.add))
            sfree(ia, ib)

        def stage_b2(l, wi):
            for j in range(4):
                slot = wget(wi + j)
                for g in range(G):
                    b = balloc()
                    mm_tm(b, [slot], 0, g)
                    if j == 0:
                        rotary(b, g, qr, R_qr[g], "cq", "sq")
                    elif j == 1:
                        rotary(b, g, kr, R_kr[g], "ck", "sk")
                    else:
                        vb = j - 2
                        S.op("act", [R_bank[b]], [R_v[g][vb]], lambda e, b=b, g=g, vb=vb: e.activation(
                            out=v_bf[:, g, vb * 512:(vb + 1) * 512], in_=banks[b][:, :], func=AF.Copy))
                    bfree(b)
                wrelease(wi + j)

        def gproj_unit(l, wi, gb_, g):
            slot = wget(wi + gb_)
            b = balloc()
            mm_tm(b, [slot], 0, g)
            i_s = salloc()
            S.op("act", [R_bank[b]], [R_scr[i_s]], lambda e: e.activation(out=scr[i_s][:], in_=banks[b][:, :], func=AF.Sigmoid))
            S.op("dve", [R_bank[b], R_scr[i_s]], [R_gg[g][gb_]], lambda e: e.tensor_tensor(
                out=gg_bf[:, g, gb_ * 512:(gb_ + 1) * 512], in0=banks[b][:, :], in1=scr[i_s][:], op=ALU.mult))
            bfree(b)
            sfree(i_s)
            if g == G - 1:
                wrelease(wi + gb_)

        def stage_ret(l, fillers=None):
            pr = l * PL + 8
            st_ = {}

            def A(g):
                k2 = g % 2
                b = balloc()
                tp = banks[b][:].bitcast(BF16)

                def tr(e):
                    ins = None
                    for h in range(H):
                        ins = e.transpose(out=tp[:, h * P:(h + 1) * P], in_=qr[:, g, h * P:(h + 1) * P], identity=ident[:])
                    for h in range(H):
                        ins = e.transpose(out=tp[:, 512 + h * P:512 + (h + 1) * P], in_=kr[:, g, h * P:(h + 1) * P], identity=ident[:])
                    return ins
                pe_group([R_qr[g], R_kr[g], R_ident], b, tr)
                S.op("act", [R_bank[b]], [R_qT[k2]], lambda e: e.activation(out=qkT[k2][:], in_=tp[:, 0:1024], func=AF.Copy))
                bfree(b)

            def B(g):
                k2 = g % 2
                b_s = balloc()

                def sc(e):
                    ins = None
                    for h in range(H):
                        ins = e.matmul(banks[b_s][:, h * P:(h + 1) * P], lhsT=qkT[k2][:, 512 + h * P:512 + (h + 1) * P],
                                       rhs=qkT[k2][:, h * P:(h + 1) * P], start=True, stop=True)
                    return ins
                pe_group([R_qT[k2]], b_s, sc)
                S.op("dve", [R_bank[b_s], R_cst], [R_sT[k2]], lambda e: e.tensor_tensor(out=sT[k2][:], in0=banks[b_s][:, :], in1=maskT, op=ALU.mult))
                bfree(b_s)

            b_S = [balloc(), balloc()]

            def Dm(g):
                for hp in range(2):
                    def spf(e, hp=hp):
                        ins = None
                        for hh in range(2):
                            h = hp * 2 + hh
                            first = (g == 0 and hh == 0)
                            ins = e.matmul(banks[b_S[hp]][:, hh * 256:(hh + 1) * 256], lhsT=kr[:, g, h * P:(h + 1) * P],
                                           rhs=v_bf[:, g, h * 256:(h + 1) * 256], start=first, stop=True, skip_group_check=(not first))
                        return ins
                    pe_group([R_kr[g], R_v[g][hp]], b_S[hp], spf)

            def C(g):
                k2 = g % 2
                b_o = [balloc(), balloc()]
                st_[("o", g)] = b_o
                sl = stloc[(g - 1) % 2]
                for hp in range(2):
                    def of(e, hp=hp):
                        ins = None
                        for hh in range(2):
                            h = hp * 2 + hh
                            o_ap = banks[b_o[hp]][:, hh * 256:(hh + 1) * 256]
                            e.matmul(o_ap, lhsT=sT[k2][:, h * P:(h + 1) * P], rhs=v_bf[:, g, h * 256:(h + 1) * 256], start=True, stop=False)
                            ins = e.matmul(o_ap, lhsT=qkT[k2][:, h * P:(h + 1) * P], rhs=stbf[l][:, h * 256:(h + 1) * 256], start=False, stop=(g == 0))
                            if g > 0:
                                ins = e.matmul(o_ap, lhsT=qkT[k2][:, h * P:(h + 1) * P], rhs=sl[:, h * 256:(h + 1) * 256], start=False, stop=True)
                        return ins
                    reads = [R_sT[k2], R_qT[k2], R_v[g][hp], R_stbf[l]] + ([R_stloc[(g - 1) % 2]] if g > 0 else [])
                    pe_group(reads, b_o[hp], of)
                for hp in range(2):
                    S.op("act", [R_bank[b_o[hp]]], [R_on[k2]], lambda e, hp=hp: e.activation(
                        out=on32[k2][:, hp * 512:(hp + 1) * 512], in_=banks[b_o[hp]][:, :], func=AF.Copy))
                    bfree(b_o[hp])

            def U(g):
                if g < G - 1:
                    sl, Rsl = stloc[g % 2], R_stloc[g % 2]
                    S.op("act", [R_bank[b_S[0]]], [Rsl], lambda e: e.activation(out=sl[:, 0:512], in_=banks[b_S[0]][:, :], func=AF.Copy))
                    S.op("act", [R_bank[b_S[1]]], [Rsl], lambda e: e.activation(out=sl[:, 512:1024], in_=banks[b_S[1]][:, :], func=AF.Copy))
                else:
                    for hp in range(2):
                        S.op("dve", [R_bank[b_S[hp]], R_st32[l]], [R_st32[l]], lambda e, hp=hp: e.tensor_tensor(
                            out=st32[l][:, 2 * hp:2 * hp + 2, :], in0=banks[b_S[hp]][:, :].rearrange("p (h v) -> p h v", h=2),
                            in1=st32[l][:, 2 * hp:2 * hp + 2, :], op=ALU.add))
                    bfree(b_S[0])
                    bfree(b_S[1])
                    for h in range(H):
                        S.op("dve", [R_st32[l]], [R_st32[l]], lambda e, h=h: e.tensor_scalar(
                            out=st32[l][:, h, :], in0=st32[l][:, h, :], scalar1=GC4[h], scalar2=None, op0=ALU.mult))
                    S.op("act", [R_st32[l]], [R_stbf[l]], lambda e: e.activation(out=stbf[l][:], in_=st32[l][:].rearrange("p h v -> p (h v)"), func=AF.Copy))

            def Ns(g):
                k2 = g % 2
                for h in range(H):
                    S.op("dve", [R_on[k2]], [R_gn], lambda e, h=h: e.bn_stats(out=bnst[:, h, :], in_=on32[k2][:, h * 256:(h + 1) * 256]))
                for h in range(H):
                    S.op("dve", [R_gn], [R_gn], lambda e, h=h: e.bn_aggr(out=mv[k2][:, h, :], in_=bnst[:, h, :]))
                S.op("dve", [R_gn], [R_gn2[k2]], lambda e: e.tensor_scalar(out=gn_r[k2][:], in0=mv[k2][:, :, 1], scalar1=EPS, scalar2=None, op0=ALU.add))
                S.op("pool", [R_gn2[k2], R_cst], [R_gn2[k2]], lambda e: e.tensor_tensor(out=gn_r[k2][:], in0=gn_r[k2][:], in1=cst[:, C_MH:C_MH + 4], op=ALU.pow))

            def Nn(g):
                k2 = g % 2
                for h in range(H):
                    S.op("dve", [R_gn2[k2], R_on[k2]], [R_on[k2]], lambda e, h=h: e.tensor_scalar(
                        out=on32[k2][:, h * 256:(h + 1) * 256], in0=on32[k2][:, h * 256:(h + 1) * 256],
                        scalar1=mv[k2][:, h, 0:1], scalar2=gn_r[k2][:, h:h + 1], op0=ALU.subtract, op1=ALU.mult))
                S.op("pool", [R_on[k2], R_gg[g][0], R_gg[g][1]], [R_gated[k2]], lambda e: e.tensor_tensor(
                    out=gated[k2][:], in0=on32[k2][:], in1=gg_bf[:, g, :], op=ALU.mult))

            def E(g):
                k2 = g % 2
                for half in range(2):
                    b = balloc()
                    tp = banks[b][:].bitcast(BF16)

                    def tr2(e, half=half, tp=tp):
                        ins = None
                        for j in range(4):
                            c = half * 4 + j
                            ins = e.transpose(out=tp[:, j * P:(j + 1) * P], in_=gated[k2][:, c * P:(c + 1) * P], identity=ident[:])
                        return ins
                    pe_group([R_gated[k2], R_ident], b, tr2)
                    S.op("dve", [R_bank[b], R_par], [R_fm[16 + half * 4 + j][g] for j in range(4)],
                         lambda e, half=half, tp=tp: e.tensor_tensor(
                             out=fm[:, 16 + half * 4:16 + half * 4 + 4, g * P:(g + 1) * P],
                             in0=tp[:, 0:512].rearrange("p (j t) -> p j t", j=4),
                             in1=par[:, pr + half * 4:pr + half * 4 + 4].unsqueeze(2).to_broadcast([P, 4, P]),
                             op=ALU.mult))
                    bfree(b)

            def F(n=1):
                for _ in range(n):
                    if fillers is not None:
                        next(fillers, None)

            A(0); A(1)
            F(2)
            B(0); Dm(0); U(0); B(1)
            C(0); A(2); F(2)
            Dm(1); U(1); B(2); Ns(0)
            C(1); A(3); F(2)
            Dm(2); U(2); B(3); Nn(0); Ns(1)
            C(2); F(2)
            Dm(3); Nn(1); Ns(2)
            E(0)
            C(3); F(2)
            Nn(2); Ns(3)
            E(1)
            U(3)
            F(2)
            Nn(3)
            E(2)
            F(4)
            E(3)
            F(1)
            if fillers is not None:
                for _ in fillers:
                    pass

        e_state = {}

        def e_pre_steps(l, wi, c):
            slot = wget(wi + c)
            b_ga = balloc()
            mm_fm(b_ga, slot, 2, 0, 0)
            iA = salloc()
            S.op("act", [R_bank[b_ga]], [R_scr[iA]], lambda e: e.activation(out=scr[iA][:], in_=banks[b_ga][:, :], func=AF.Sigmoid))
            bfree(b_ga)
            yield
            b_yc = balloc()
            mm_fm(b_yc, slot, 0, 0, 8)
            S.op("dve", [R_bank[b_yc], R_scr[iA]], [R_scr[iA]], lambda e: e.tensor_tensor(out=scr[iA][:], in0=banks[b_yc][:, :], in1=scr[iA][:], op=ALU.mult))
            bfree(b_yc)
            yield
            b_gb = balloc()
            mm_fm(b_gb, slot, 3, 0, 0)
            iB = salloc()
            S.op("act", [R_bank[b_gb]], [R_scr[iB]], lambda e: e.activation(out=scr[iB][:], in_=banks[b_gb][:, :], func=AF.Sigmoid))
            bfree(b_gb)
            e_state[c] = (slot, iA, iB)
            yield

        def e_post(l, wi, c):
            if c not in e_state:
                for _ in e_pre_steps(l, wi, c):
                    pass
            slot, iA, iB = e_state.pop(c)
            b_yr = balloc()
            mm_fm(b_yr, slot, 1, 0, 16)
            S.op("dve", [R_bank[b_yr], R_scr[iB]], [R_scr[iB]], lambda e: e.tensor_tensor(out=scr[iB][:], in0=banks[b_yr][:, :], in1=scr[iB][:], op=ALU.mult))
            bfree(b_yr)
            S.op("pool", [R_scr[iA], R_scr[iB]], R_fm[24 + c], lambda e: e.tensor_tensor(
                out=fm[:, 24 + c, :], in0=scr[iA][:], in1=scr[iB][:], op=ALU.add))
            sfree(iA, iB)
            wrelease(wi + c)

        def stage_e(l, wi):
            for c in range(8):
                e_post(l, wi, c)

        def run_tail(main, norm, dep):
            if norm is None:
                for m in main:
                    m()
                return
            pre, tr = norm
            if dep:
                seq = [main[0], main[1], main[2], main[3], main[4], lambda: pre(0), main[5], lambda: pre(1), lambda: tr(0),
                       main[6], lambda: pre(2), lambda: tr(1), main[7], lambda: pre(3), lambda: tr(2), lambda: tr(3)]
            else:
                seq = [main[0], lambda: pre(0), main[1], lambda: pre(1), main[2], lambda: tr(0), main[3], lambda: pre(2),
                       main[4], lambda: tr(1), main[5], lambda: pre(3), main[6], lambda: tr(2), main[7], lambda: tr(3)]
            for f in seq:
                f()

        def stage_wo(xb, l, wi, norm):
            x_sb, R_x = x_sbs[xb], R_xs2[xb]
            slots = {}
            pre_b = {}

            def M(cb, g):
                def f():
                    if g == 0:
                        slots[cb] = wget(wi + cb)
                    if cb == 0 and g == 0:
                        b = balloc()
                        mm_tm(b, [slots[0]], 24, 0, part=0)
                        pre_b[1] = balloc()
                        mm_tm(pre_b[1], [slots[0]], 24, 1, part=0)
                        mm_tm(b, [slots[0]], 24, 0, part=1)
                    elif cb == 0 and g == 1:
                        b = pre_b.pop(1)
                        mm_tm(b, [slots[0]], 24, 1, part=1)
                    else:
                        b = balloc()
                        mm_tm(b, [slots[cb]], 24, g)
                    S.op("dve", [R_bank[b], R_x[g][cb]], [R_x[g][cb]], lambda e: e.tensor_tensor(
                        out=x_sb[:, g, cb * 512:(cb + 1) * 512], in0=banks[b][:, :], in1=x_sb[:, g, cb * 512:(cb + 1) * 512], op=ALU.add))
                    bfree(b)
                    if g == G - 1:
                        wrelease(wi + cb)
                return f
            run_tail([M(cb, g) for cb in range(2) for g in range(G)], norm, True)
            rms_stats2(xb)

        def stage_mlp(xb, l, wi, norm, dep, mid_hook=None):
            x_sb, R_x = x_sbs[xb], R_xs2[xb]
            for hf in range(2):
                base = wi + hf * 8
                for j in range(16):
                    blk = base + j // 4
                    slot = wget(blk)
                    b = balloc()
                    mm_fm(b, slot, j % 4, 0, 0, split=(j == 0))
                    i_r = salloc()
                    S.op("act", [R_bank[b]], [R_scr[i_r]], lambda e, b=b, i=i_r: e.activation(out=scr[i][:], in_=banks[b][:, :], func=AF.Relu))
                    bfree(b)
                    eng = "pool" if j % 2 == 0 else "dve"
                    S.op(eng, [R_scr[i_r]], R_fm[8 + j], lambda e, i=i_r, j=j: e.tensor_tensor(out=fm[:, 8 + j, :], in0=scr[i][:], in1=scr[i][:], op=ALU.mult))
                    sfree(i_r)
                    if j % 4 == 3:
                        wrelease(blk)
                slots = {}
                pre_b = {}

                def M(cb, g, base=base, slots=slots, pre_b=pre_b):
                    def f():
                        if g == 0:
                            slots[cb] = (wget(base + 4 + cb * 2), wget(base + 4 + cb * 2 + 1))
                        if cb == 0 and g == 0:
                            b = balloc()
                            mm_tm(b, list(slots[0]), 8, 0, part=0)
                            pre_b[1] = balloc()
                            mm_tm(pre_b[1], list(slots[0]), 8, 1, part=0)
                            mm_tm(b, list(slots[0]), 8, 0, part=1)
                        elif cb == 0 and g == 1:
                            b = pre_b.pop(1)
                            mm_tm(b, list(slots[0]), 8, 1, part=1)
                        else:
                            b = balloc()
                            mm_tm(b, list(slots[cb]), 8, g)
                        S.op("dve", [R_bank[b], R_x[g][cb], R_ssq2], [R_x[g][cb]], lambda e: e.scalar_tensor_tensor(
                            out=x_sb[:, g, cb * 512:(cb + 1) * 512], in0=banks[b][:, :], scalar=rstd2[:, g:g + 1],
                            in1=x_sb[:, g, cb * 512:(cb + 1) * 512], op0=ALU.mult, op1=ALU.add))
                        bfree(b)
                        if g == G - 1:
                            wrelease(base + 4 + cb * 2)
                            wrelease(base + 4 + cb * 2 + 1)
                    return f
                main = [M(cb, g) for cb in range(2) for g in range(G)]
                if hf == 0 and mid_hook is not None:
                    main[0]()
                    main[1]()
                    mid_hook()
                    main = main[2:]
                    for m in main:
                        m()
                else:
                    run_tail(main, norm if hf == 1 else None, dep)

        def stage_final(it):
            xb = it % 2
            x_sb, R_x = x_sbs[xb], R_xs2[xb]
            for g in range(G):
                k = g % 2
                rms_stats(xb, g)
                S.op("act", R_x[g] + [R_ssq], [R_on[k]], lambda e, g=g, k=k: e.activation(
                    out=on32[k][:], in_=x_sb[:, g, :], func=AF.Copy, scale=rstd[:, g:g + 1]))
                S.op("dve", [R_on[k], R_par], [R_on[k]], lambda e, k=k: e.tensor_tensor(out=on32[k][:], in0=on32[k][:], in1=par[:, P_NF:P_NF + D], op=ALU.mult))
                r0 = it * T + g * P
                S.dma("sp", "out%d" % k, [R_on[k]], [R_out], [lambda e, k=k, r0=r0: e.dma_start(out=out_d[r0:r0 + P, :], in_=on32[k][:])])

        def load_x(it):
            xb = it % 2
            S.dma("sp", "xld", [], [r for rr in R_xs2[xb] for r in rr], [
                lambda e, g=g: e.dma_start(out=x_sbs[xb][:, g, :], in_=x_d[it * T + g * P: it * T + (g + 1) * P, :]) for g in range(G)])

        wi = 0
        for it in range(ntiles):
            xb = it % 2
            if it == 0:
                load_x(0)
                tables_a(0)
                tables_b(0)
                norm_full(xb, 0 * PL + 0)
            for l in range(nlayers):
                for c in range(4):
                    conv_chunk(l, c, wi + B_CONVA)
                if l == 0 and it + 1 < ntiles:
                    load_x(it + 1)
                stage_b2(l, wi + B_B2)
                for c in range(4, 8):
                    conv_chunk(l, c, wi + B_CONVB)
                if l == nlayers - 1 and it + 1 < ntiles:
                    tables_a(it + 1)

                def fill_gen(l=l, wi=wi):
                    for g in range(G):
                        for gb_ in range(2):
                            gproj_unit(l, wi + B_G, gb_, g)
                            yield
                    for c in range(3):
                        for _ in e_pre_steps(l, wi + B_E, c):
                            yield
                stage_ret(l, fill_gen())
                stage_e(l, wi + B_E)
                stage_wo(xb, l, wi + B_WO, make_norm(xb, l * PL + 16, raw=True))
                if l < nlayers - 1:
                    stage_mlp(xb, l, wi + B_MLP, make_norm(xb, (l + 1) * PL + 0), True)
                elif it + 1 < ntiles:
                    stage_mlp(xb, l, wi + B_MLP, make_norm((it + 1) % 2, 0), False, mid_hook=lambda: tables_b(it + 1))
                else:
                    stage_mlp(xb, l, wi + B_MLP, None, True)
                wi += NBLK
            stage_final(it)

        ensure_cast(ncast_total - 1)
        if dbg_d is not None:
            pass
        for key in list(S.dlast.keys()):
            S._wait("sp", S.dlast[key])
        build.stats = dict(nops=dict(S.nops), nwaits=S.nwaits)
    return nc


def make_consts():
    c = np.zeros((P, NCONST), np.float64)
    c[:, C_ID:C_ID + 128] = np.eye(128)
    j = np.arange(128, dtype=np.float64)
    for h in range(H):
        gam = 1.0 - 2.0 ** (-5.0 - h)
        gC = gam ** 128.0
        for g in range(G):
            c[:, C_DQ + h * 4 + g] = gam ** (j + 1.0) * gC ** g * (128.0 ** -0.5)
            c[:, C_DK + h * 4 + g] = gam ** (127.0 - j) * gC ** (-(g + 1.0))
        m = (j[None, :] >= j[:, None]).astype(np.float64)
        c[:, C_MASK + h * 128:C_MASK + (h + 1) * 128] = m
    c[0, C_INVF:C_INVF + 64] = 10000.0 ** (-np.arange(64, dtype=np.float64) / 64.0)
    c[:, C_MH:C_MH + 4] = -0.5
    return c.astype(np.float32)


def make_params(norm_mix, conv_w, ret_norm, norm_mlp, norm_final):
    p = np.zeros((P, NPAR), np.float32)
    for l in range(L):
        o = l * PL
        p[:, o + 0:o + 8] = np.asarray(norm_mix[l]).reshape(8, P).T
        p[:, o + 8:o + 16] = np.asarray(ret_norm[l]).reshape(8, P).T
        p[:, o + 16:o + 24] = np.asarray(norm_mlp[l]).reshape(8, P).T
        p[:, o + 24:o + 48] = np.asarray(conv_w[l]).reshape(3, 8, P).transpose(2, 1, 0).reshape(P, 24)
    p[:, P_NF:P_NF + D] = np.asarray(norm_final)[None, :]
    return p


_NC_CACHE = {}


def kernel(x, positions, norm_mix, w_in, conv_w, w_conv_out, ret_norm, w_ret_out, w_o, norm_mlp, w_up, w_down, norm_final):
    x = np.asarray(x, np.float32)
    B, S_, _ = x.shape
    ntiles = S_ // T
    key = (ntiles,)
    if key not in _NC_CACHE:
        _NC_CACHE[key] = build(ntiles=ntiles)
    nc = _NC_CACHE[key]
    cst = make_consts()
    par = make_params(norm_mix, conv_w, ret_norm, norm_mlp, norm_final)
    shared = {
        "cst": cst, "par": par,
        "w_in": np.ascontiguousarray(w_in, np.float32), "w_conv_out": np.ascontiguousarray(w_conv_out, np.float32),
        "w_ret_out": np.ascontiguousarray(w_ret_out, np.float32), "w_o": np.ascontiguousarray(w_o, np.float32),
        "w_up": np.ascontiguousarray(w_up, np.float32), "w_down": np.ascontiguousarray(w_down, np.float32),
    }
    pos = np.asarray(positions, np.int32)
    in_maps = []
    for b in range(B):
        m = dict(shared)
        m["x"] = np.ascontiguousarray(x[b])
        m["pos"] = np.ascontiguousarray(pos[b][None, :])
        in_maps.append(m)
    res = run_bass_kernel_spmd(nc, in_maps, core_ids=list(range(B)))
    return np.stack([np.asarray(r["out"]) for r in res.results], axis=0).astype(np.float32)
```
